# Optimizing a Trainium2 kernel written in Bass

```python
import math
import jax, jax.numpy as jnp
from jax import lax
import numpy as np

D_MODEL = 2048
BATCH = 2
SEQ = 8192
DEPTH = 2

N_A_LAYERS = DEPTH // 2
N_B_LAYERS = DEPTH - N_A_LAYERS

SSM_EXPAND = 2
D_INNER = SSM_EXPAND * D_MODEL
SSM_HEAD_DIM = 64
SSM_HEADS = D_INNER // SSM_HEAD_DIM
SSM_GROUPS = 8
SSM_HEADS_PER_GROUP = SSM_HEADS // SSM_GROUPS
SSM_STATE = 128
CONV_K = 4
SSD_CHUNK = 256
CONV_DIM = D_INNER + 2 * SSM_GROUPS * SSM_STATE
A_IN_DIM = D_INNER + CONV_DIM + SSM_HEADS
GNORM_GROUPS = SSM_GROUPS

ATT_HEADS = 16
ATT_HEAD_DIM = D_MODEL // ATT_HEADS
ATT_WIDTH = ATT_HEADS * ATT_HEAD_DIM
B_IN_DIM = 2 * ATT_WIDTH
KV_DIM = 2 * ATT_WIDTH
MOBA_BLOCK = 256
MOBA_TOPK = 3
Q_CHUNK = 32
ROPE_THETA = 10000.0
EPS = 1e-5

kernel_name = "yoco_mamba2_moba_hybrid"


def rmsnorm(x, g):
    xf = x.astype(jnp.float32)
    y = xf * lax.rsqrt(jnp.mean(xf * xf, axis=-1, keepdims=True) + EPS)
    return (y * g.astype(jnp.float32)).astype(x.dtype)


def grouped_rmsnorm(y, g, n_groups):
    shp = y.shape
    yf = y.astype(jnp.float32).reshape(shp[:-1] + (n_groups, shp[-1] // n_groups))
    yf = yf * lax.rsqrt(jnp.mean(yf * yf, axis=-1, keepdims=True) + EPS)
    return (yf.reshape(shp) * g.astype(jnp.float32)).astype(y.dtype)


def rope(x):
    s, hd = x.shape[1], x.shape[-1]
    inv_freq = ROPE_THETA ** (-jnp.arange(0, hd, 2, dtype=jnp.float32) / hd)
    ang = jnp.arange(s, dtype=jnp.float32)[:, None] * inv_freq[None, :]
    cos = jnp.cos(ang)[None, :, None, :]
    sin = jnp.sin(ang)[None, :, None, :]
    xf = x.astype(jnp.float32)
    x1, x2 = xf[..., : hd // 2], xf[..., hd // 2:]
    return jnp.concatenate([x1 * cos - x2 * sin, x2 * cos + x1 * sin], axis=-1).astype(x.dtype)


def causal_depthwise_conv(u, w, b):
    c = u.shape[-1]
    out = lax.conv_general_dilated(
        u, w.astype(u.dtype)[:, None, :], window_strides=(1,), padding=[(CONV_K - 1, 0)],
        dimension_numbers=("NWC", "WIO", "NWC"), feature_group_count=c)
    return out + b.astype(u.dtype)


def ssd_chunked_scan(x_dt, a, bm, cm):
    bsz, s = x_dt.shape[0], x_dt.shape[1]
    t = -(-s // SSD_CHUNK) * SSD_CHUNK
    pad = t - s
    nc = t // SSD_CHUNK

    def to_chunks(u):
        u = jnp.pad(u, [(0, 0), (0, pad)] + [(0, 0)] * (u.ndim - 2))
        u = u.reshape((bsz, nc, SSD_CHUNK) + u.shape[2:])
        return jnp.moveaxis(u, 1, 0)

    xs = to_chunks(x_dt.astype(jnp.float32))
    As = to_chunks(a.astype(jnp.float32).reshape(bsz, s, SSM_GROUPS, SSM_HEADS_PER_GROUP))
    Bs = to_chunks(bm.astype(jnp.float32))
    Cs = to_chunks(cm.astype(jnp.float32))
    causal = jnp.tril(jnp.ones((SSD_CHUNK, SSD_CHUNK), dtype=bool))[None, :, :, None, None]

    def step(state, inp):
        xc, ac, bc, cc = inp
        a_cum = jnp.cumsum(ac, axis=1)
        seg = a_cum[:, :, None] - a_cum[:, None, :]
        decay = jnp.exp(jnp.where(causal, seg, -jnp.inf))
        cb = jnp.einsum("blgn,bsgn->blsg", cc, bc)
        y_diag = jnp.einsum("blsgr,bsgrp->blgrp", cb[..., None] * decay, xc)
        y_off = jnp.einsum("blgn,bgrpn->blgrp", cc, state) * jnp.exp(a_cum)[..., None]
        decay_end = jnp.exp(a_cum[:, -1:] - a_cum)
        new_state = state * jnp.exp(a_cum[:, -1])[..., None, None] + jnp.einsum(
            "bsgn,bsgr,bsgrp->bgrpn", bc, decay_end, xc)
        return new_state, y_diag + y_off

    state0 = jnp.zeros((bsz, SSM_GROUPS, SSM_HEADS_PER_GROUP, SSM_HEAD_DIM, SSM_STATE), jnp.float32)
    _, ys = lax.scan(step, state0, (xs, As, Bs, Cs))
    ys = jnp.moveaxis(ys, 0, 1).reshape((bsz, t) + ys.shape[3:])
    return ys[:, :s].astype(x_dt.dtype)


def mamba2_layer(h, w_in, conv_w, conv_b, dt_bias, A_log, D_skip, gnorm_g, w_out):
    bsz, s, _ = h.shape
    proj = h @ w_in
    z = proj[..., :D_INNER]
    xbc = proj[..., D_INNER:D_INNER + CONV_DIM]
    dt = proj[..., D_INNER + CONV_DIM:]
    xbc = jax.nn.silu(causal_depthwise_conv(xbc, conv_w, conv_b))
    xs = xbc[..., :D_INNER].reshape(bsz, s, SSM_GROUPS, SSM_HEADS_PER_GROUP, SSM_HEAD_DIM)
    bm = xbc[..., D_INNER:D_INNER + SSM_GROUPS * SSM_STATE].reshape(bsz, s, SSM_GROUPS, SSM_STATE)
    cm = xbc[..., D_INNER + SSM_GROUPS * SSM_STATE:].reshape(bsz, s, SSM_GROUPS, SSM_STATE)
    dt = jax.nn.softplus(dt.astype(jnp.float32) + dt_bias.astype(jnp.float32))
    A = -jnp.exp(A_log.astype(jnp.float32))
    a = dt * A
    x_dt = xs * dt.reshape(bsz, s, SSM_GROUPS, SSM_HEADS_PER_GROUP)[..., None].astype(xs.dtype)
    y = ssd_chunked_scan(x_dt, a, bm, cm)
    y = y + D_skip.reshape(SSM_GROUPS, SSM_HEADS_PER_GROUP)[:, :, None].astype(xs.dtype) * xs
    y = y.reshape(bsz, s, D_INNER)
    y = grouped_rmsnorm(y * jax.nn.silu(z), gnorm_g, GNORM_GROUPS)
    return y @ w_out


def shared_kv(x, kv_norm_g, w_kv):
    bsz, s, _ = x.shape
    kv = rmsnorm(x, kv_norm_g) @ w_kv
    k = rope(kv[..., :ATT_WIDTH].reshape(bsz, s, ATT_HEADS, ATT_HEAD_DIM))
    v = kv[..., ATT_WIDTH:].reshape(bsz, s, ATT_HEADS, ATT_HEAD_DIM)
    nb = -(-s // MOBA_BLOCK)
    pad = nb * MOBA_BLOCK - s

    def to_blocks(u):
        u = jnp.pad(jnp.transpose(u, (0, 2, 1, 3)), [(0, 0), (0, 0), (0, pad), (0, 0)])
        return u.reshape(bsz, ATT_HEADS, nb, MOBA_BLOCK, ATT_HEAD_DIM)

    kb, vb = to_blocks(k), to_blocks(v)
    kbar = jnp.mean(kb.astype(jnp.float32), axis=3).astype(kb.dtype)
    return kb, vb, kbar


def moba_attention(q, kb, vb, kbar):
    bsz, nh, s, hd = q.shape
    nb = kb.shape[2]
    n_sel = min(MOBA_TOPK, nb)
    scale = ATT_HEAD_DIM ** -0.5
    q_blk = jnp.arange(s) // MOBA_BLOCK
    gate = jnp.einsum("bhsd,bhnd->bhsn", q, kbar).astype(jnp.float32)
    past = jnp.arange(nb)[None, :] < q_blk[:, None]
    gate = jnp.where(past, gate, -jnp.inf)
    _, sel = lax.top_k(gate, n_sel)
    sel_ok = sel < q_blk[:, None]
    bi = jnp.arange(bsz)[:, None, None, None]
    hi = jnp.arange(nh)[None, :, None, None]

    def one_chunk(c):
        start = c * Q_CHUNK
        qc = lax.dynamic_slice_in_dim(q, start, Q_CHUNK, axis=2).astype(jnp.float32)
        idx = lax.dynamic_slice_in_dim(sel, start, Q_CHUNK, axis=2)
        ok = lax.dynamic_slice_in_dim(sel_ok, start, Q_CHUNK, axis=2)
        own = start // MOBA_BLOCK
        k_sel = kb[bi, hi, idx]
        v_sel = vb[bi, hi, idx]
        k_own = lax.dynamic_index_in_dim(kb, own, axis=2, keepdims=False)
        v_own = lax.dynamic_index_in_dim(vb, own, axis=2, keepdims=False)
        s_sel = jnp.einsum("bhqd,bhqjkd->bhqjk", qc, k_sel) * scale
        s_sel = jnp.where(ok[..., None], s_sel, -jnp.inf).reshape(bsz, nh, Q_CHUNK, n_sel * MOBA_BLOCK)
        s_own = jnp.einsum("bhqd,bhkd->bhqk", qc, k_own) * scale
        q_pos = start + jnp.arange(Q_CHUNK)
        k_pos = own * MOBA_BLOCK + jnp.arange(MOBA_BLOCK)
        s_own = jnp.where(k_pos[None, :] <= q_pos[:, None], s_own, -jnp.inf)
        p = jax.nn.softmax(jnp.concatenate([s_sel, s_own], axis=-1), axis=-1)
        p_sel = p[..., : n_sel * MOBA_BLOCK].reshape(bsz, nh, Q_CHUNK, n_sel, MOBA_BLOCK)
        p_own = p[..., n_sel * MOBA_BLOCK:]
        out = jnp.einsum("bhqjk,bhqjkd->bhqd", p_sel, v_sel) + jnp.einsum("bhqk,bhkd->bhqd", p_own, v_own)
        return out.astype(q.dtype)

    outs = lax.map(one_chunk, jnp.arange(s // Q_CHUNK))
    return jnp.transpose(outs, (1, 0, 3, 2, 4)).reshape(bsz, s, nh, hd)


def moba_layer(h, w_in, w_out, kb, vb, kbar):
    bsz, s, _ = h.shape
    proj = h @ w_in
    q = rope(proj[..., :ATT_WIDTH].reshape(bsz, s, ATT_HEADS, ATT_HEAD_DIM))
    z = proj[..., ATT_WIDTH:]
    o = moba_attention(jnp.transpose(q, (0, 2, 1, 3)), kb, vb, kbar).reshape(bsz, s, ATT_WIDTH)
    return (o * jax.nn.silu(z)) @ w_out


def setup_inputs(seed: int = 0) -> dict:
    key = jax.random.key(seed)
    ks = jax.random.split(key, 20)
    f32 = jnp.float32

    def nrm(k, shape, scale):
        return jax.random.normal(k, shape, f32) * scale

    def gain(k, shape):
        return 1.0 + 0.01 * jax.random.normal(k, shape, f32)

    dt = jnp.exp(jax.random.uniform(ks[5], (N_A_LAYERS, SSM_HEADS), f32)
                 * (math.log(0.1) - math.log(0.001)) + math.log(0.001))
    dt_bias = dt + jnp.log(-jnp.expm1(-dt))
    return {
        "x": jax.random.normal(ks[0], (BATCH, SEQ, D_MODEL), f32),
        "a_norm_g": gain(ks[1], (N_A_LAYERS, D_MODEL)),
        "a_w_in": nrm(ks[2], (N_A_LAYERS, D_MODEL, A_IN_DIM), D_MODEL ** -0.5),
        "a_conv_w": nrm(ks[3], (N_A_LAYERS, CONV_K, CONV_DIM), CONV_K ** -0.5),
        "a_conv_b": nrm(ks[4], (N_A_LAYERS, CONV_DIM), 0.01),
        "a_dt_bias": dt_bias,
        "a_A_log": jnp.log(jax.random.uniform(ks[6], (N_A_LAYERS, SSM_HEADS), f32, 1.0, 16.0)),
        "a_D": 1.0 + 0.1 * jax.random.normal(ks[7], (N_A_LAYERS, SSM_HEADS), f32),
        "a_gnorm_g": gain(ks[8], (N_A_LAYERS, D_INNER)),
        "a_w_out": nrm(ks[9], (N_A_LAYERS, D_INNER, D_MODEL), D_INNER ** -0.5),
        "kv_norm_g": gain(ks[10], (D_MODEL,)),
        "w_kv": nrm(ks[11], (D_MODEL, KV_DIM), D_MODEL ** -0.5),
        "b_norm_g": gain(ks[12], (N_B_LAYERS, D_MODEL)),
        "b_w_in": nrm(ks[13], (N_B_LAYERS, D_MODEL, B_IN_DIM), D_MODEL ** -0.5),
        "b_w_out": nrm(ks[14], (N_B_LAYERS, ATT_WIDTH, D_MODEL), ATT_WIDTH ** -0.5),
        "final_norm_g": gain(ks[15], (D_MODEL,)),
    }


def reference(x, a_norm_g, a_w_in, a_conv_w, a_conv_b, a_dt_bias, a_A_log, a_D, a_gnorm_g, a_w_out,
              kv_norm_g, w_kv, b_norm_g, b_w_in, b_w_out, final_norm_g):
    kb = vb = kbar = None
    for i in range(DEPTH):
        if i < N_A_LAYERS:
            h = rmsnorm(x, a_norm_g[i])
            x = x + mamba2_layer(h, a_w_in[i], a_conv_w[i], a_conv_b[i], a_dt_bias[i], a_A_log[i],
                                 a_D[i], a_gnorm_g[i], a_w_out[i])
        else:
            if i == N_A_LAYERS:
                kb, vb, kbar = shared_kv(x, kv_norm_g, w_kv)
            j = i - N_A_LAYERS
            h = rmsnorm(x, b_norm_g[j])
            x = x + moba_layer(h, b_w_in[j], b_w_out[j], kb, vb, kbar)
    return rmsnorm(x, final_norm_g)
```

```python
import ml_dtypes
import numpy as np
import concourse.bass as bass
import concourse.mybir as mybir
from concourse.bass_utils import run_bass_kernel_spmd

F32 = mybir.dt.float32
BF16 = mybir.dt.bfloat16
U32 = mybir.dt.uint32
I32 = mybir.dt.int32
AF = mybir.ActivationFunctionType
ALU = mybir.AluOpType
AX = mybir.AxisListType

ENGS = ("pe", "dve", "act", "pool", "sp")


class T:
    def __init__(self, ctx, name, t, excl=False):
        self.ctx = ctx
        self.excl = excl
        self.name = name
        self.t = t
        self.last_write = None
        self.readers = {}
        self._sem = None
        self._semcnt = 0

    def __getitem__(self, idx):
        return self.t[idx]

    def dsem(self):
        if self._sem is None:
            self._sem = self.ctx.new_sem("d_" + self.name)
        return self._sem


class V:
    def __init__(self, T_, sl):
        self.T_ = T_
        self.sl = sl

    def __getitem__(self, idx):
        return self.T_[:, self.sl][idx]


class Ctx:
    def __init__(self, nc, same_engine_sync=True):
        self.nc = nc
        self.q = {e: [] for e in ENGS}
        self.cnt = {e: 0 for e in ENGS}
        self.sems = {}
        self.semh = {}
        for e in ENGS:
            self.semh[e] = nc.alloc_semaphore("s_" + e)
        self.waited = {e: {} for e in ENGS}
        self.same = same_engine_sync
        self.nsem = len(ENGS)
        self.uid = 0

    def new_sem(self, name):
        self.nsem += 1
        h = self.nc.alloc_semaphore(name)
        self.semh[name] = h
        return name

    def sb(self, name, shape, dt):
        return T(self, name, self.nc.alloc_sbuf_tensor(name, list(shape), dt))

    def ps(self, name, shape, dt=F32):
        return T(self, name, self.nc.alloc_psum_tensor(name, list(shape), dt), excl=True)

    def dram(self, name, shape, dt, kind=None):
        if kind is None:
            t = self.nc.dram_tensor(name, list(shape), dt)
        else:
            t = self.nc.dram_tensor(name, list(shape), dt, kind=kind)
        return T(self, name, t)

    def view(self, name, t):
        return T(self, name, t)

    def _deps(self, reads, writes):
        deps = {}
        writes = list(writes) + [t for t in reads if t.excl]
        reads = [t for t in reads if not t.excl]
        for t in reads:
            if t.last_write is not None:
                k, v = t.last_write
                deps[k] = max(deps.get(k, 0), v)
        for t in writes:
            if t.last_write is not None:
                k, v = t.last_write
                deps[k] = max(deps.get(k, 0), v)
            for k, v in t.readers.items():
                deps[k] = max(deps.get(k, 0), v)
        return deps

    def _emit_waits(self, e, deps, same):
        for k, v in deps.items():
            if k == e and not same:
                continue
            if self.waited[e].get(k, 0) < v:
                self.waited[e][k] = v
                h = self.semh[k]
                self.q[e].append(lambda eng, h=h, v=v: eng.wait_ge(h, v))

    def op(self, e, fn, reads=(), writes=(), same=None):
        if same is None:
            same = self.same and e != "pe"
        reads = [getattr(t, "T_", t) for t in reads]
        writes = [getattr(t, "T_", t) for t in writes]
        deps = self._deps(reads, writes)
        self._emit_waits(e, deps, same)
        self.cnt[e] += 1
        n = self.cnt[e]
        h = self.semh[e]
        self.q[e].append(lambda eng, fn=fn, h=h: fn(eng).then_inc(h, 1))
        for t in reads:
            if t.excl:
                t.last_write = (e, n)
                t.readers = {}
            else:
                t.readers[e] = max(t.readers.get(e, 0), n)
        for t in writes:
            t.last_write = (e, n)
            t.readers = {}
        return n

    def dma(self, e, pairs, reads=(), writes=(), semT=None, **kw):
        reads = [getattr(t, "T_", t) for t in reads]
        writes = [getattr(t, "T_", t) for t in writes]
        deps = self._deps(reads, writes)
        self._emit_waits(e, deps, True)
        if semT is None:
            semT = writes[0] if writes else reads[0]
        k = semT.dsem()
        h = self.semh[k]
        for (o, i) in pairs:
            self.q[e].append(lambda eng, o=o, i=i, h=h, kw=kw: eng.dma_start(out=o, in_=i, **kw).then_inc(h, 16))
        semT._semcnt += 16 * len(pairs)
        v = semT._semcnt
        for t in reads:
            t.readers[k] = max(t.readers.get(k, 0), v)
        for t in writes:
            t.last_write = (k, v)
            t.readers = {}
        return (k, v)

    def wait_all(self, e, ts):
        deps = {}
        for t in ts:
            if t.last_write is not None:
                k, v = t.last_write
                deps[k] = max(deps.get(k, 0), v)
            for k, v in t.readers.items():
                deps[k] = max(deps.get(k, 0), v)
        self._emit_waits(e, deps, True)

    def finish(self):
        nc = self.nc
        emap = {"pe": "tensor", "dve": "vector", "act": "scalar", "pool": "gpsimd", "sp": "sync"}
        with nc.Block() as block:
            for e in ENGS:
                thunks = self.q[e]
                if not thunks:
                    continue

                def body(eng, thunks=thunks):
                    for th in thunks:
                        th(eng)
                getattr(block, emap[e])(body)
        return nc


def pcol(v, n):
    return np.ascontiguousarray(np.asarray(v, np.float32).reshape(n, 128).T)


def prep_s1(inp, core, NT=8192):
    b, j = core // 4, core % 4
    g0 = 2 * j
    w = inp["a_w_in"][0]
    W1 = np.concatenate([
        w[:, 4096 + g0 * 512: 4096 + g0 * 512 + 1024],
        w[:, 8192 + g0 * 128: 8192 + g0 * 128 + 256],
        w[:, 9216 + g0 * 128: 9216 + g0 * 128 + 256],
        w[:, g0 * 512: g0 * 512 + 1024],
        w[:, 10240 + g0 * 8: 10240 + g0 * 8 + 16]], axis=1)
    chans = np.concatenate([np.arange(g0 * 512, g0 * 512 + 1024),
                            4096 + np.arange(g0 * 128, g0 * 128 + 256),
                            5120 + np.arange(g0 * 128, g0 * 128 + 256)])
    cwv = inp["a_conv_w"][0][:, chans]
    cw = np.ascontiguousarray(cwv.reshape(4, 12, 128).transpose(2, 1, 0).reshape(128, 48))
    cb = pcol(inp["a_conv_b"][0][chans], 12)
    hs = slice(g0 * 8, g0 * 8 + 16)
    dtb = np.ascontiguousarray(np.tile(inp["a_dt_bias"][0][hs][None, :], (128, 2)))
    alog = np.ascontiguousarray(np.tile(inp["a_A_log"][0][hs][None, :], (128, 2)))
    D = inp["a_D"][0][hs]
    dcol = np.ascontiguousarray(np.repeat(D.reshape(8, 2), 64, axis=1).T)
    gn = pcol(inp["a_gnorm_g"][0][g0 * 512: g0 * 512 + 1024], 8)
    ang = pcol(inp["a_norm_g"][0], 16)
    xT = np.ascontiguousarray(inp["x"][b, :NT].T)
    return {"xT": xT, "W1": np.ascontiguousarray(W1), "cw": cw, "cb": cb, "dtb": dtb.astype(np.float32),
            "alog": alog.astype(np.float32), "dcol": dcol.astype(np.float32), "gn": gn, "ang": ang}


def rope_tables(NT=8192):
    inv = (10000.0 ** (-np.arange(0, 128, 2, dtype=np.float32) / 128)).astype(np.float32)
    ang = np.arange(NT, dtype=np.float32)[:, None] * inv[None, :]
    cos = np.cos(ang).astype(np.float32).T
    sin = np.sin(ang).astype(np.float32).T
    cosT = np.ascontiguousarray(np.concatenate([cos, cos], axis=0))
    sinT = np.ascontiguousarray(np.concatenate([-sin, sin], axis=0))
    return cosT, sinT


def prep_w3(inp, j, NH=4):
    wkv = inp["w_kv"]
    wb = inp["b_w_in"][0]
    out = np.empty((NH, 2048, 768), np.float32)
    for i in range(NH):
        h = 4 * j + i
        cs = slice(h * 128, (h + 1) * 128)
        k = wkv[:, cs]
        q = wb[:, cs]
        out[i, :, 0:128] = k
        out[i, :, 128:256] = np.concatenate([k[:, 64:], k[:, :64]], axis=1)
        out[i, :, 256:384] = q
        out[i, :, 384:512] = np.concatenate([q[:, 64:], q[:, :64]], axis=1)
        out[i, :, 512:640] = wkv[:, 2048 + h * 128: 2048 + (h + 1) * 128]
        out[i, :, 640:768] = wb[:, 2048 + h * 128: 2048 + (h + 1) * 128]
    return out


EPS = 1e-5
BIG = 1e30


def build_s1(NT=8192):
    nc = bass.Bass("TRN2", target_bir_lowering=False)
    c = Ctx(nc)
    NCH = NT // 256
    xT = c.dram("xT", [2048, NT], F32, kind="ExternalInput")
    W1 = c.dram("W1", [2048, 2576], F32, kind="ExternalInput")
    cwd = c.dram("cw", [128, 48], F32, kind="ExternalInput")
    cbd = c.dram("cb", [128, 12], F32, kind="ExternalInput")
    dtbd = c.dram("dtb", [128, 32], F32, kind="ExternalInput")
    alogd = c.dram("alog", [128, 32], F32, kind="ExternalInput")
    dcold = c.dram("dcol", [128, 8], F32, kind="ExternalInput")
    gnd = c.dram("gn", [128, 8], F32, kind="ExternalInput")
    angd = c.dram("ang", [128, 16], F32, kind="ExternalInput")
    ynT = c.dram("ynT", [1024, NT], BF16, kind="ExternalOutput")

    W1b = c.sb("W1b", [128, 16, 2576], BF16)
    cw = c.sb("cws", [128, 48], F32)
    cb = c.sb("cbs", [128, 12], F32)
    dtb = c.sb("dtbs", [128, 32], F32)
    Aneg = c.sb("Aneg", [128, 32], F32)
    dcol = c.sb("dcols", [128, 8], F32)
    gn = c.sb("gns", [128, 8], F32)
    ang = c.sb("angs", [128, 16], F32)
    ones = c.sb("ones", [128, 128], F32)
    triA = c.sb("triA", [128, 256], F32)
    triB = c.sb("triB", [128, 256], F32)
    ident = c.sb("ident", [128, 128], BF16)

    c.dma("sp", [(cw[:], cwd[:]), (cb[:], cbd[:]), (dtb[:], dtbd[:]), (Aneg[:], alogd[:]),
                 (dcol[:], dcold[:]), (gn[:], gnd[:]), (ang[:], angd[:])],
          writes=[cw, cb, dtb, Aneg, dcol, gn, ang])
    pairs = []
    for ck in range(16):
        for hf in range(2):
            c0, c1 = hf * 1288, (hf + 1) * 1288
            pairs.append((W1b[:, ck, c0:c1], W1[ck * 128:(ck + 1) * 128, c0:c1]))
    c.dma("pool", pairs, writes=[W1b])
    c.op("pool", lambda e: e.memset(ones[:], 1.0), writes=[ones])
    c.op("pool", lambda e: e.memset(triA[:], 1.0), writes=[triA])
    c.op("pool", lambda e: e.affine_select(out=triA[:, 0:128], in_=triA[:, 0:128], pattern=[[1, 128]],
                                           compare_op=ALU.is_ge, fill=0.0, base=0, channel_multiplier=-1),
         reads=[triA], writes=[triA])
    c.op("pool", lambda e: e.memset(triB[:], 0.0), writes=[triB])
    c.op("pool", lambda e: e.tensor_copy(out=triB[:, 128:256], in_=triA[:, 0:128]), reads=[triA], writes=[triB])
    c.op("pool", lambda e: e.memset(ident[:], 1.0), writes=[ident])
    c.op("pool", lambda e: e.affine_select(out=ident[:], in_=ident[:], pattern=[[-1, 128]],
                                           compare_op=ALU.is_equal, fill=0.0, base=0, channel_multiplier=1),
         reads=[ident], writes=[ident])
    c.op("act", lambda e: e.activation(out=Aneg[:], in_=Aneg[:], func=AF.Exp), reads=[Aneg], writes=[Aneg])
    c.op("dve", lambda e: e.tensor_scalar(out=Aneg[:], in0=Aneg[:], scalar1=-1.0, scalar2=None, op0=ALU.mult),
         reads=[Aneg], writes=[Aneg])

    xTs = c.sb("xTs", [128, 16, 256], F32)
    sqr = [c.sb(f"sqr{i}", [128, 256], F32) for i in range(2)]
    rtmp = c.sb("rtmp", [128, 256], F32)
    rstd = c.sb("rstd", [128, 256], F32)
    hT = c.sb("hT", [128, 16, 256], BF16)
    u = c.sb("u", [128, 12, 259], F32)
    accr = [c.sb(f"acc{i}", [128, 256], F32) for i in range(2)]
    xcf = c.sb("xcf", [128, 8, 256], F32)
    xcb = c.sb("xcb", [128, 8, 256], BF16)
    BCb = c.sb("BCb", [128, 4, 256], BF16)
    zs = c.sb("zs", [128, 8, 256], F32)
    xtok = [c.sb(f"xtok{i}", [128, 1024], BF16) for i in range(2)]
    Btok = [c.sb(f"Btok{i}", [128, 256], BF16) for i in range(2)]
    xsc = [[c.sb(f"xsc{t}_{g}", [128, 512], BF16) for g in range(2)] for t in range(2)]
    dv = c.sb("dv", [128, 32], F32)
    dabs = c.sb("dabs", [128, 32], F32)
    dl = c.sb("dl", [128, 32], F32)
    dts = c.sb("dts", [128, 32], F32)
    asb = c.sb("asb", [128, 32], F32)
    lndt = c.sb("lndt", [128, 32], F32)
    bias = c.sb("bias", [128, 32], F32)
    cbm = [c.sb(f"cbm{g}", [128, 2, 256], F32) for g in range(2)]
    arep = [c.sb(f"arep{i}", [128, 2, 128], F32) for i in range(2)]
    E0 = [c.sb(f"E0_{i}", [128, 256], F32) for i in range(2)]
    E1 = [c.sb(f"E1_{i}", [128, 128], F32) for i in range(2)]
    Er = [c.sb(f"Er_{i}", [128, 256], F32) for i in range(2)]
    W0 = [c.sb(f"W0_{i}", [128, 256], BF16) for i in range(2)]
    Wt1 = [c.sb(f"Wt1_{i}", [128, 128], BF16) for i in range(2)]
    Cr = [c.sb(f"Cr_{i}", [128, 256], BF16) for i in range(2)]
    edec = [c.sb(f"edec{g}", [128, 8], F32) for g in range(2)]
    S = [c.sb(f"S{g}", [128, 512], F32) for g in range(2)]
    Sb = [c.sb(f"Sb{g}", [128, 512], BF16) for g in range(2)]
    ysb = [c.sb(f"ysb{i}", [128, 256], F32) for i in range(2)]
    yg = [c.sb(f"yg{g}", [128, 4, 256], F32) for g in range(2)]
    rg = c.sb("rg", [128, 256], F32)
    rgt = c.sb("rgt", [128, 256], F32)
    ynb = [c.sb(f"ynb{i}", [128, 8, 256], BF16) for i in range(2)]

    pj = [c.ps(f"pj{i}", [128, 512]) for i in range(2)]
    bmisc = c.ps("bmisc", [128, 512])
    bT = c.ps("bT", [128, 1024], BF16)
    bcb = c.ps("bcb", [128, 512])
    p_cum = [c.ps(f"pcum{i}", [128, 512]) for i in range(2)]
    bY = c.ps("bY", [128, 512])

    p_dt, p_ct, p_ss = V(bmisc, slice(0, 32)), V(bmisc, slice(32, 64)), V(bmisc, slice(256, 512))
    p_cb = p_S = bcb
    p_T = [V(bT, slice(0, 512)), V(bT, slice(512, 1024))]
    p_y = [V(bY, slice(0, 256)), V(bY, slice(256, 512))]

    for g in range(2):
        c.op("pool", lambda e, g=g: e.memset(S[g][:], 0.0), writes=[S[g]])
        c.op("pool", lambda e, g=g: e.memset(Sb[g][:], 0.0), writes=[Sb[g]])
    c.op("pool", lambda e: e.memset(u[:], 0.0), writes=[u])

    pjc = 0
    sqc = 0
    for t in range(NCH):
        t0 = t * 256
        src = xT[:, t0:t0 + 256].rearrange("(c p) t -> p c t", p=128)
        c.dma("sp", [(xTs[:, 4 * i:4 * i + 4, :], src[:, 4 * i:4 * i + 4, :]) for i in range(4)], writes=[xTs])
        for ck in range(16):
            sq = sqr[sqc % 2]; sqc += 1
            c.op("act", lambda e, sq=sq, ck=ck: e.activation(out=sq[:], in_=xTs[:, ck, :], func=AF.Square),
                 reads=[xTs], writes=[sq])
            c.op("pe", lambda e, sq=sq, ck=ck: e.matmul(p_ss[:], lhsT=ones[:], rhs=sq[:], start=(ck == 0), stop=(ck == 15)),
                 reads=[ones, sq], writes=[p_ss])
        c.op("act", lambda e: e.activation(out=rtmp[:], in_=p_ss[:], func=AF.Sqrt, scale=1.0 / 2048, bias=EPS),
             reads=[p_ss], writes=[rtmp])
        c.op("dve", lambda e: e.reciprocal(out=rstd[:], in_=rtmp[:]), reads=[rtmp], writes=[rstd])
        for ck in range(16):
            c.op("dve", lambda e, ck=ck: e.scalar_tensor_tensor(out=hT[:, ck, :], in0=xTs[:, ck, :], scalar=ang[:, ck:ck + 1],
                                                                in1=rstd[:], op0=ALU.mult, op1=ALU.mult),
                 reads=[xTs, ang, rstd], writes=[hT])
        for ot in range(20):
            pp = pj[pjc % 2]; pjc += 1
            for ck in range(16):
                c.op("pe", lambda e, pp=pp, ot=ot, ck=ck: e.matmul(pp[:, 0:256], lhsT=W1b[:, ck, ot * 128:(ot + 1) * 128], rhs=hT[:, ck, :],
                                                                   start=(ck == 0), stop=(ck == 15)),
                     reads=[W1b, hT], writes=[pp])
            if ot < 12:
                c.op("act", lambda e, pp=pp, ot=ot: e.copy(out=u[:, ot, 3:259], in_=pp[:, 0:256]), reads=[pp], writes=[u])
            else:
                c.op("act", lambda e, pp=pp, ot=ot: e.activation(out=zs[:, ot - 12, :], in_=pp[:, 0:256], func=AF.Silu),
                     reads=[pp], writes=[zs])
        for tt in range(2):
            for ck in range(16):
                c.op("pe", lambda e, tt=tt, ck=ck: e.matmul(p_dt[:, tt * 16:(tt + 1) * 16], lhsT=hT[:, ck, tt * 128:(tt + 1) * 128],
                                                            rhs=W1b[:, ck, 2560:2576], start=(ck == 0), stop=(ck == 15)),
                     reads=[W1b, hT], writes=[p_dt])
        c.op("dve", lambda e: e.tensor_tensor(out=dv[:], in0=p_dt[:], in1=dtb[:], op=ALU.add),
             reads=[p_dt, dtb], writes=[dv])
        c.op("act", lambda e: e.activation(out=dabs[:], in_=dv[:], func=AF.Abs), reads=[dv], writes=[dabs])
        c.op("act", lambda e: e.activation(out=dabs[:], in_=dabs[:], func=AF.Exp, scale=-1.0), reads=[dabs], writes=[dabs])
        c.op("act", lambda e: e.activation(out=dl[:], in_=dabs[:], func=AF.Ln, bias=1.0), reads=[dabs], writes=[dl])
        c.op("dve", lambda e: e.scalar_tensor_tensor(out=dts[:], in0=dv[:], scalar=0.0, in1=dl[:], op0=ALU.max, op1=ALU.add),
             reads=[dv, dl], writes=[dts])
        c.op("act", lambda e: e.activation(out=lndt[:], in_=dts[:], func=AF.Ln), reads=[dts], writes=[lndt])
        c.op("dve", lambda e: e.tensor_tensor(out=asb[:], in0=dts[:], in1=Aneg[:], op=ALU.mult),
             reads=[dts, Aneg], writes=[asb])
        c.op("pe", lambda e: e.matmul(p_ct[:, 0:16], lhsT=triA[:, 0:128], rhs=asb[:, 0:16], start=True, stop=True),
             reads=[triA, asb], writes=[p_ct])
        c.op("pe", lambda e: e.matmul(p_ct[:, 16:32], lhsT=ones[:], rhs=asb[:, 0:16], start=True, stop=False),
             reads=[ones, asb], writes=[p_ct])
        c.op("pe", lambda e: e.matmul(p_ct[:, 16:32], lhsT=triA[:, 0:128], rhs=asb[:, 16:32], start=False, stop=True),
             reads=[triA, asb], writes=[p_ct])
        c.op("dve", lambda e: e.tensor_tensor(out=bias[:], in0=lndt[:], in1=p_ct[:], op=ALU.subtract),
             reads=[lndt, p_ct], writes=[bias])
        for ot in range(12):
            acc = accr[ot % 2]
            c.op("dve", lambda e, acc=acc, ot=ot: e.tensor_scalar(out=acc[:], in0=u[:, ot, 3:259], scalar1=cw[:, ot * 4 + 3:ot * 4 + 4],
                                                                  scalar2=cb[:, ot:ot + 1], op0=ALU.mult, op1=ALU.add),
                 reads=[u, cw, cb], writes=[acc])
            for k in range(3):
                c.op("dve", lambda e, acc=acc, ot=ot, k=k: e.scalar_tensor_tensor(out=acc[:], in0=u[:, ot, k:k + 256],
                                                                                 scalar=cw[:, ot * 4 + k:ot * 4 + k + 1], in1=acc[:],
                                                                                 op0=ALU.mult, op1=ALU.add),
                     reads=[u, cw, acc], writes=[acc])
            if ot < 8:
                c.op("act", lambda e, acc=acc, ot=ot: e.activation(out=xcf[:, ot, :], in_=acc[:], func=AF.Silu),
                     reads=[acc], writes=[xcf])
            else:
                c.op("act", lambda e, acc=acc, ot=ot: e.activation(out=BCb[:, ot - 8, :], in_=acc[:], func=AF.Silu),
                     reads=[acc], writes=[BCb])
        c.op("pool", lambda e: e.tensor_copy(out=xcb[:], in_=xcf[:]), reads=[xcf], writes=[xcb])
        c.op("pool", lambda e: e.tensor_copy(out=u[:, :, 0:3], in_=u[:, :, 256:259]), reads=[u], writes=[u])
        for tt in range(2):
            for hf in range(2):
                for i in range(4):
                    ti = hf * 4 + i
                    c.op("pe", lambda e, tt=tt, hf=hf, i=i, ti=ti: e.transpose(p_T[hf][:, i * 128:(i + 1) * 128],
                                                                               xcb[:, ti, tt * 128:(tt + 1) * 128], ident[:]),
                         reads=[xcb, ident], writes=[p_T[hf]])
                c.op("dve", lambda e, tt=tt, hf=hf: e.tensor_copy(out=xtok[tt][:, hf * 512:(hf + 1) * 512], in_=p_T[hf][:]),
                     reads=[p_T[hf]], writes=[xtok[tt]])
            for g in range(2):
                c.op("pe", lambda e, tt=tt, g=g: e.transpose(p_T[0][:, g * 128:(g + 1) * 128], BCb[:, g, tt * 128:(tt + 1) * 128], ident[:]),
                     reads=[BCb, ident], writes=[p_T[0]])
            c.op("dve", lambda e, tt=tt: e.tensor_copy(out=Btok[tt][:], in_=p_T[0][:, 0:256]), reads=[p_T[0]], writes=[Btok[tt]])
        yo = ynb[t % 2]
        for g in range(2):
            BT = lambda sl, g=g: BCb[:, g, sl]
            for st in range(2):
                c.op("pe", lambda e, g=g, st=st: e.matmul(p_cb[:, st * 256:(st + 1) * 256], lhsT=BCb[:, g, st * 128:(st + 1) * 128],
                                                          rhs=BCb[:, 2 + g, :], start=True, stop=True),
                     reads=[BCb], writes=[p_cb])
            c.op("dve", lambda e, g=g: e.tensor_tensor(out=cbm[g][:, 0, :], in0=p_cb[:, 0:256], in1=triA[:], op=ALU.mult),
                 reads=[p_cb, triA], writes=[cbm[g]])
            c.op("dve", lambda e, g=g: e.tensor_tensor(out=cbm[g][:, 1, 128:256], in0=p_cb[:, 384:512], in1=triA[:, 0:128], op=ALU.mult),
                 reads=[p_cb, triA], writes=[cbm[g]])
            for r in range(8):
                hh = g * 8 + r
                par = hh % 2
                ar, pc = arep[par], p_cum[par]
                e0, e1, er, w0, w1, cr = E0[par], E1[par], Er[par], W0[par], Wt1[par], Cr[par]
                for st in range(2):
                    c.op("pool", lambda e, ar=ar, st=st, hh=hh: e.tensor_scalar(out=ar[:, st, :], in0=ones[:], scalar1=asb[:, st * 16 + hh:st * 16 + hh + 1],
                                                                                scalar2=0.0, op0=ALU.mult, op1=ALU.add),
                         reads=[ones, asb], writes=[ar])
                c.op("pe", lambda e, ar=ar, pc=pc: e.matmul(pc[:, 0:256], lhsT=ar[:, 0, :], rhs=triA[:], start=True, stop=False),
                     reads=[ar, triA], writes=[pc])
                c.op("pe", lambda e, ar=ar, pc=pc: e.matmul(pc[:, 0:256], lhsT=ar[:, 1, :], rhs=triB[:], start=False, stop=True),
                     reads=[ar, triB], writes=[pc])
                c.op("act", lambda e, e0=e0, pc=pc, hh=hh: e.activation(out=e0[:], in_=pc[:, 0:256], func=AF.Exp, bias=bias[:, hh:hh + 1]),
                     reads=[pc, bias], writes=[e0])
                c.op("act", lambda e, e1=e1, pc=pc, hh=hh: e.activation(out=e1[:], in_=pc[:, 128:256], func=AF.Exp, bias=bias[:, 16 + hh:16 + hh + 1]),
                     reads=[pc, bias], writes=[e1])
                c.op("act", lambda e, er=er, pc=pc: e.activation(out=er[:], in_=pc[:, 0:256], func=AF.Exp), reads=[pc], writes=[er])
                c.op("act", lambda e, pc=pc, g=g, r=r: e.activation(out=edec[g][:, r:r + 1], in_=pc[:, 255:256], func=AF.Exp),
                     reads=[pc], writes=[edec[g]])
                c.op("dve", lambda e, w0=w0, e0=e0, g=g: e.scalar_tensor_tensor(out=w0[:], in0=e0[:], scalar=BIG, in1=cbm[g][:, 0, :],
                                                                                op0=ALU.min, op1=ALU.mult),
                     reads=[e0, cbm[g]], writes=[w0])
                c.op("dve", lambda e, w1=w1, e1=e1, g=g: e.scalar_tensor_tensor(out=w1[:], in0=e1[:], scalar=BIG, in1=cbm[g][:, 1, 128:256],
                                                                                op0=ALU.min, op1=ALU.mult),
                     reads=[e1, cbm[g]], writes=[w1])
                c.op("pool", lambda e, cr=cr, er=er, g=g: e.tensor_tensor(out=cr[:], in0=BCb[:, 2 + g, :], in1=er[:], op=ALU.mult),
                     reads=[BCb, er], writes=[cr])
                c.op("pool", lambda e, e0=e0, g=g, r=r, hh=hh: e.tensor_scalar(out=xsc[0][g][:, r * 64:(r + 1) * 64],
                                                                               in0=xtok[0][:, hh * 64:(hh + 1) * 64], scalar1=e0[:, 255:256],
                                                                               scalar2=0.0, op0=ALU.mult, op1=ALU.add),
                     reads=[xtok[0], e0], writes=[xsc[0][g]])
                c.op("pool", lambda e, e1=e1, g=g, r=r, hh=hh: e.tensor_scalar(out=xsc[1][g][:, r * 64:(r + 1) * 64],
                                                                               in0=xtok[1][:, hh * 64:(hh + 1) * 64], scalar1=e1[:, 127:128],
                                                                               scalar2=0.0, op0=ALU.mult, op1=ALU.add),
                     reads=[xtok[1], e1], writes=[xsc[1][g]])
                py = p_y[(hh // 2) % 2]
                po = (r % 2) * 64
                c.op("pe", lambda e, py=py, po=po, w0=w0, hh=hh: e.matmul(py[po:po + 64, :], lhsT=xtok[0][:, hh * 64:(hh + 1) * 64], rhs=w0[:],
                                                                          start=True, stop=False),
                     reads=[xtok[0], w0], writes=[py])
                c.op("pe", lambda e, py=py, po=po, w1=w1, hh=hh: e.matmul(py[po:po + 64, 128:256], lhsT=xtok[1][:, hh * 64:(hh + 1) * 64], rhs=w1[:],
                                                                          start=False, stop=False),
                     reads=[xtok[1], w1], writes=[py])
                c.op("pe", lambda e, py=py, po=po, cr=cr, g=g, r=r: e.matmul(py[po:po + 64, :], lhsT=Sb[g][:, r * 64:(r + 1) * 64], rhs=cr[:],
                                                                             start=False, stop=True),
                     reads=[Sb[g], cr], writes=[py])
                if r % 2 == 1:
                    ti = r // 2
                    gt = g * 4 + ti
                    ys = ysb[ti % 2]
                    c.op("dve", lambda e, ys=ys, py=py, gt=gt: e.scalar_tensor_tensor(out=ys[:], in0=xcf[:, gt, :], scalar=dcol[:, gt:gt + 1],
                                                                                      in1=py[:], op0=ALU.mult, op1=ALU.add),
                         reads=[xcf, dcol, py], writes=[ys])
                    c.op("pool", lambda e, ys=ys, g=g, ti=ti, gt=gt: e.tensor_tensor(out=yg[g][:, ti, :], in0=ys[:], in1=zs[:, gt, :], op=ALU.mult),
                         reads=[ys, zs], writes=[yg[g]])
            c.op("pe", lambda e, g=g: e.matmul(p_S[:], lhsT=Btok[0][:, g * 128:(g + 1) * 128], rhs=xsc[0][g][:], start=True, stop=False),
                 reads=[Btok[0], xsc[0][g]], writes=[p_S])
            c.op("pe", lambda e, g=g: e.matmul(p_S[:], lhsT=Btok[1][:, g * 128:(g + 1) * 128], rhs=xsc[1][g][:], start=False, stop=True),
                 reads=[Btok[1], xsc[1][g]], writes=[p_S])
            for r in range(8):
                c.op("dve", lambda e, g=g, r=r: e.scalar_tensor_tensor(out=S[g][:, r * 64:(r + 1) * 64], in0=S[g][:, r * 64:(r + 1) * 64],
                                                                       scalar=edec[g][:, r:r + 1], in1=p_S[:, r * 64:(r + 1) * 64],
                                                                       op0=ALU.mult, op1=ALU.add),
                     reads=[S[g], edec[g], p_S], writes=[S[g]])
            c.op("pool", lambda e, g=g: e.tensor_copy(out=Sb[g][:], in_=S[g][:]), reads=[S[g]], writes=[Sb[g]])
            for ti in range(4):
                sq = sqr[sqc % 2]; sqc += 1
                c.op("act", lambda e, sq=sq, g=g, ti=ti: e.activation(out=sq[:], in_=yg[g][:, ti, :], func=AF.Square),
                     reads=[yg[g]], writes=[sq])
                c.op("pe", lambda e, sq=sq, ti=ti: e.matmul(p_ss[:], lhsT=ones[:], rhs=sq[:], start=(ti == 0), stop=(ti == 3)),
                     reads=[ones, sq], writes=[p_ss])
            c.op("act", lambda e: e.activation(out=rgt[:], in_=p_ss[:], func=AF.Sqrt, scale=1.0 / 512, bias=EPS),
                 reads=[p_ss], writes=[rgt])
            c.op("dve", lambda e: e.reciprocal(out=rg[:], in_=rgt[:]), reads=[rgt], writes=[rg])
            for ti in range(4):
                gt = g * 4 + ti
                c.op("dve", lambda e, g=g, ti=ti, gt=gt, yo=yo: e.scalar_tensor_tensor(out=yo[:, gt, :], in0=yg[g][:, ti, :], scalar=gn[:, gt:gt + 1],
                                                                                      in1=rg[:], op0=ALU.mult, op1=ALU.mult),
                     reads=[yg[g], gn, rg], writes=[yo])
        dst = ynT[:, t0:t0 + 256].rearrange("(c p) t -> p c t", p=128)
        c.dma("sp", [(dst, yo[:])], reads=[yo], writes=[ynT], semT=yo)
    c.wait_all("sp", [ynb[0], ynb[1]])
    for e in ("pe", "dve", "act", "pool"):
        pass
    return c.finish()


def build_proj(KC, mode, NTOK=2048, TB=256):
    nc = bass.Bass("TRN2", target_bir_lowering=False)
    c = Ctx(nc)
    NB = NTOK // TB
    aT = c.dram("aT", [KC * 128, NTOK], BF16, kind="ExternalInput")
    W = c.dram("W", [KC * 128, 2048], F32, kind="ExternalInput")
    rT = c.dram("rT", [2048, NTOK], F32, kind="ExternalInput")
    g1d = c.dram("g1", [128, 16], F32, kind="ExternalInput")
    g1 = c.sb("g1s", [128, 16], F32)
    if mode == "s2":
        g2d = c.dram("g2", [128, 16], F32, kind="ExternalInput")
        g2 = c.sb("g2s", [128, 16], F32)
        x1T = c.dram("x1T", [2048, NTOK], F32, kind="ExternalOutput")
        hkvT = c.dram("hkvT", [2048, NTOK], BF16, kind="ExternalOutput")
        hbT = c.dram("hbT", [2048, NTOK], BF16, kind="ExternalOutput")
        c.dma("sp", [(g1[:], g1d[:]), (g2[:], g2d[:])], writes=[g1, g2])
    else:
        outT = c.dram("outT", [2048, NTOK], F32, kind="ExternalOutput")
        c.dma("sp", [(g1[:], g1d[:])], writes=[g1])

    Wb = c.sb("Wb", [128, KC, 2048], BF16)
    ones = c.sb("ones", [128, 128], F32)
    pairs = []
    for ck in range(KC):
        pairs.append((Wb[:, ck, :], W[ck * 128:(ck + 1) * 128, :]))
    c.dma("pool", pairs, writes=[Wb])
    c.op("pool", lambda e: e.memset(ones[:], 1.0), writes=[ones])

    aTs = c.sb("aTs", [128, KC, TB], BF16)
    xs = c.sb("xs", [128, 16, TB], F32)
    sqr = [c.sb(f"sqr{i}", [128, TB], F32) for i in range(2)]
    rtmp = c.sb("rtmp", [128, TB], F32)
    rstd = c.sb("rstd", [128, TB], F32)
    if mode == "s2":
        o1 = [c.sb(f"o1_{i}", [128, 16, TB], BF16) for i in range(1)]
        o2 = [c.sb(f"o2_{i}", [128, 16, TB], BF16) for i in range(1)]
    else:
        o1 = [c.sb(f"o1_{i}", [128, 16, TB], F32) for i in range(1)]
    pj = [c.ps(f"pj{i}", [128, 512]) for i in range(2)]
    pss = c.ps("pss", [128, 512])

    def blk(d, b):
        return d[:, b * TB:(b + 1) * TB].rearrange("(c p) t -> p c t", p=128)

    sqc = 0
    pjc = 0
    for b in range(NB):
        sa = blk(aT, b)
        nsp = max(1, KC // 8)
        c.dma("sp", [(aTs[:, i * 8:(i + 1) * 8, :], sa[:, i * 8:(i + 1) * 8, :]) for i in range(nsp)], writes=[aTs])
        sr = blk(rT, b)
        c.dma("sp", [(xs[:, i * 8:(i + 1) * 8, :], sr[:, i * 8:(i + 1) * 8, :]) for i in range(2)], writes=[xs])
        for ft in range(16):
            pp = pj[pjc % 2]; pjc += 1
            for ck in range(KC):
                c.op("pe", lambda e, pp=pp, ft=ft, ck=ck: e.matmul(pp[:, 0:TB], lhsT=Wb[:, ck, ft * 128:(ft + 1) * 128], rhs=aTs[:, ck, :],
                                                                   start=(ck == 0), stop=(ck == KC - 1)),
                     reads=[Wb, aTs], writes=[pp])
            c.op("dve", lambda e, pp=pp, ft=ft: e.tensor_tensor(out=xs[:, ft, :], in0=pp[:, 0:TB], in1=xs[:, ft, :], op=ALU.add),
                 reads=[pp, xs], writes=[xs])
            sq = sqr[sqc % 2]; sqc += 1
            c.op("act", lambda e, sq=sq, ft=ft: e.activation(out=sq[:], in_=xs[:, ft, :], func=AF.Square), reads=[xs], writes=[sq])
            c.op("pe", lambda e, sq=sq, ft=ft: e.matmul(pss[:, 0:TB], lhsT=ones[:], rhs=sq[:], start=(ft == 0), stop=(ft == 15)),
                 reads=[ones, sq], writes=[pss])
        c.op("act", lambda e: e.activation(out=rtmp[:], in_=pss[:, 0:TB], func=AF.Sqrt, scale=1.0 / 2048, bias=EPS),
             reads=[pss], writes=[rtmp])
        c.op("dve", lambda e: e.reciprocal(out=rstd[:], in_=rtmp[:]), reads=[rtmp], writes=[rstd])
        oa = o1[0]
        for ft in range(16):
            c.op("dve", lambda e, ft=ft, oa=oa: e.scalar_tensor_tensor(out=oa[:, ft, :], in0=xs[:, ft, :], scalar=g1[:, ft:ft + 1], in1=rstd[:],
                                                                       op0=ALU.mult, op1=ALU.mult),
                 reads=[xs, g1, rstd], writes=[oa])
        if mode == "s2":
            ob = o2[0]
            for ft in range(16):
                c.op("dve", lambda e, ft=ft, ob=ob: e.scalar_tensor_tensor(out=ob[:, ft, :], in0=xs[:, ft, :], scalar=g2[:, ft:ft + 1], in1=rstd[:],
                                                                           op0=ALU.mult, op1=ALU.mult),
                     reads=[xs, g2, rstd], writes=[ob])
            c.dma("sp", [(blk(x1T, b), xs[:])], reads=[xs], writes=[x1T], semT=xs)
            c.dma("sp", [(blk(hkvT, b), oa[:])], reads=[oa], writes=[hkvT], semT=oa)
            c.dma("sp", [(blk(hbT, b), ob[:])], reads=[ob], writes=[hbT], semT=ob)
        else:
            c.dma("sp", [(blk(outT, b), oa[:])], reads=[oa], writes=[outT], semT=oa)
    if mode == "s2":
        c.wait_all("sp", [xs, o1[0], o2[0]])
    else:
        c.wait_all("sp", [o1[0]])
    return c.finish()


BIGM = 30000.0
SCALE = 128 ** -0.5


def build_s3(NT=8192, NH=4):
    nc = bass.Bass("TRN2", target_bir_lowering=False)
    c = Ctx(nc)
    NBLK = NT // 256
    NTB = NT // 512
    hkvT = c.dram("hkvT", [2048, NT], BF16, kind="ExternalInput")
    hbT = c.dram("hbT", [2048, NT], BF16, kind="ExternalInput")
    W3 = c.dram("W3", [NH, 2048, 768], F32, kind="ExternalInput")
    cosd = c.dram("cosT", [128, NT], F32, kind="ExternalInput")
    sind = c.dram("sinT", [128, NT], F32, kind="ExternalInput")
    ogT = c.dram("ogT", [NH * 128, NT], BF16, kind="ExternalOutput")

    onesb = c.sb("onesb", [128, 128], BF16)
    identb = c.sb("identb", [128, 128], BF16)
    Eall = c.sb("Eall", [32, 32, 128], BF16)
    cm0 = c.sb("cm0", [128, 256], BF16)
    cm1 = c.sb("cm1", [128, 256], BF16)
    c.op("pool", lambda e: e.memset(onesb[:], 1.0), writes=[onesb])
    c.op("pool", lambda e: e.memset(identb[:], 1.0), writes=[identb])
    c.op("pool", lambda e: e.affine_select(out=identb[:], in_=identb[:], pattern=[[-1, 128]], compare_op=ALU.is_equal,
                                           fill=0.0, base=0, channel_multiplier=1), reads=[identb], writes=[identb])
    c.op("pool", lambda e: e.memset(Eall[:], BIGM), writes=[Eall])
    c.op("pool", lambda e: e.affine_select(out=Eall[:], in_=Eall[:], pattern=[[-1, 32], [0, 128]], compare_op=ALU.is_equal,
                                           fill=0.0, base=0, channel_multiplier=1), reads=[Eall], writes=[Eall])
    c.op("pool", lambda e: e.memset(cm0[:], 0.0), writes=[cm0])
    c.op("pool", lambda e: e.affine_select(out=cm0[:], in_=cm0[:], pattern=[[1, 256]], compare_op=ALU.is_ge,
                                           fill=-BIGM, base=0, channel_multiplier=-1), reads=[cm0], writes=[cm0])
    c.op("pool", lambda e: e.memset(cm1[:], 0.0), writes=[cm1])
    c.op("pool", lambda e: e.affine_select(out=cm1[:], in_=cm1[:], pattern=[[1, 256]], compare_op=ALU.is_ge,
                                           fill=-BIGM, base=-128, channel_multiplier=-1), reads=[cm1], writes=[cm1])

    Wh = c.sb("Wh", [128, 16, 768], BF16)
    hkv = c.sb("hkv", [128, 16, 512], BF16)
    hb = c.sb("hb", [128, 16, 512], BF16)
    cosb = c.sb("cosb", [128, 512], F32)
    sinb = c.sb("sinb", [128, 512], F32)
    KT = c.sb("KT", [128, NT], BF16)
    QT = c.sb("QT", [128, NT], BF16)
    Vt = c.sb("Vt", [128, NT // 128, 128], BF16)
    ZS = c.sb("ZS", [128, NT], BF16)
    t1 = [c.sb(f"t1_{i}", [128, 512], F32) for i in range(2)]
    t2 = [c.sb(f"t2_{i}", [128, 512], F32) for i in range(2)]
    kr = c.sb("kr", [128, 512], F32)
    kbs = c.sb("kbs", [128, NBLK], F32)
    kbb = c.sb("kbb", [128, 32], BF16)
    gm = c.sb("gm", [128, 2, 32], F32)
    m8 = c.sb("m8", [128, 2, 8], F32)
    selb = c.sb("selb", [128, 2, 32], BF16)
    selT = c.sb("selT", [32, 256], BF16)
    PT = [c.sb(f"PT{i}", [128, 2, 256], BF16) for i in range(3)]
    rinv = c.sb("rinv", [128, 256], F32)
    on = c.sb("on", [128, 256], F32)
    ob = [c.sb(f"ob{i}", [128, 1024], BF16) for i in range(2)]

    bk = [c.ps(f"bk{i}", [128, 512]) for i in range(7)]
    bkT = c.ps("bkT", [128, 1024], BF16)

    def tblk(d, tb):
        return d[:, tb * 512:(tb + 1) * 512].rearrange("(c p) t -> p c t", p=128)

    obc = 0
    ptc = 0
    for h in range(NH):
        c.dma("pool", [(Wh[:, ck, :], W3[h, ck * 128:(ck + 1) * 128, :]) for ck in range(16)], writes=[Wh])
        c.op("pool", lambda e: e.memset(kbb[:], 0.0), writes=[kbb])
        c.op("pool", lambda e: e.memset(gm[:], -1e30), writes=[gm])
        for tb in range(NTB):
            s_kv, s_b = tblk(hkvT, tb), tblk(hbT, tb)
            c.dma("sp", [(hkv[:, 0:8, :], s_kv[:, 0:8, :]), (hkv[:, 8:16, :], s_kv[:, 8:16, :])], writes=[hkv])
            c.dma("sp", [(hb[:, 0:8, :], s_b[:, 0:8, :]), (hb[:, 8:16, :], s_b[:, 8:16, :])], writes=[hb])
            c.dma("sp", [(cosb[:], cosd[:, tb * 512:(tb + 1) * 512])], writes=[cosb])
            c.dma("sp", [(sinb[:], sind[:, tb * 512:(tb + 1) * 512])], writes=[sinb])
            ts = slice(tb * 512, (tb + 1) * 512)
            for which, src, dstT, w0 in (("k", hkv, KT, 0), ("q", hb, QT, 256)):
                pa, pb = (bk[0], bk[1]) if which == "k" else (bk[2], bk[3])
                for ck in range(16):
                    c.op("pe", lambda e, pa=pa, src=src, ck=ck, w0=w0: e.matmul(pa[:], lhsT=Wh[:, ck, w0:w0 + 128], rhs=src[:, ck, :],
                                                                                start=(ck == 0), stop=(ck == 15)),
                         reads=[Wh, src], writes=[pa])
                for ck in range(16):
                    c.op("pe", lambda e, pb=pb, src=src, ck=ck, w0=w0: e.matmul(pb[:], lhsT=Wh[:, ck, w0 + 128:w0 + 256], rhs=src[:, ck, :],
                                                                                start=(ck == 0), stop=(ck == 15)),
                         reads=[Wh, src], writes=[pb])
                i = 0 if which == "k" else 1
                c.op("dve", lambda e, pa=pa, i=i: e.tensor_tensor(out=t1[i][:], in0=pa[:], in1=cosb[:], op=ALU.mult),
                     reads=[pa, cosb], writes=[t1[i]])
                c.op("dve", lambda e, pb=pb, i=i: e.tensor_tensor(out=t2[i][:], in0=pb[:], in1=sinb[:], op=ALU.mult),
                     reads=[pb, sinb], writes=[t2[i]])
                if which == "k":
                    c.op("pool", lambda e, i=i: e.tensor_tensor(out=kr[:], in0=t1[i][:], in1=t2[i][:], op=ALU.add),
                         reads=[t1[i], t2[i]], writes=[kr])
                    c.op("act", lambda e, ts=ts: e.copy(out=KT[:, ts], in_=kr[:]), reads=[kr], writes=[KT])
                    c.op("dve", lambda e, tb=tb: e.tensor_reduce(out=kbs[:, tb * 2:tb * 2 + 2], in_=kr[:].rearrange("p (a b) -> p a b", a=2),
                                                                 axis=AX.X, op=ALU.add),
                         reads=[kr], writes=[kbs])
                else:
                    c.op("pool", lambda e, i=i, ts=ts: e.tensor_tensor(out=QT[:, ts], in0=t1[i][:], in1=t2[i][:], op=ALU.add),
                         reads=[t1[i], t2[i]], writes=[QT])
            for tt in range(4):
                for ck in range(16):
                    c.op("pe", lambda e, tt=tt, ck=ck: e.matmul(bk[4][:, tt * 128:(tt + 1) * 128], lhsT=hkv[:, ck, tt * 128:(tt + 1) * 128],
                                                                rhs=Wh[:, ck, 512:640], start=(ck == 0), stop=(ck == 15)),
                         reads=[Wh, hkv], writes=[bk[4]])
            c.op("act", lambda e, tb=tb: e.copy(out=Vt[:, tb * 4:(tb + 1) * 4, :], in_=bk[4][:].rearrange("p (a b) -> p a b", a=4)),
                 reads=[bk[4]], writes=[Vt])
            for ck in range(16):
                c.op("pe", lambda e, ck=ck: e.matmul(bk[5][:], lhsT=Wh[:, ck, 640:768], rhs=hb[:, ck, :], start=(ck == 0), stop=(ck == 15)),
                     reads=[Wh, hb], writes=[bk[5]])
            c.op("act", lambda e, ts=ts: e.activation(out=ZS[:, ts], in_=bk[5][:], func=AF.Silu), reads=[bk[5]], writes=[ZS])
        c.op("act", lambda e: e.activation(out=kbb[:, 0:NBLK], in_=kbs[:], func=AF.Copy, scale=1.0 / 256), reads=[kbs], writes=[kbb])
        for qb in range(NBLK):
            qs = slice(qb * 256, (qb + 1) * 256)
            if qb > 0:
                for qt in range(2):
                    c.op("pe", lambda e, qt=qt, qb=qb: e.matmul(bk[2][:, qt * 32:(qt + 1) * 32], lhsT=QT[:, qb * 256 + qt * 128:qb * 256 + (qt + 1) * 128],
                                                                rhs=kbb[:], start=True, stop=True),
                         reads=[QT, kbb], writes=[bk[2]])
                c.op("dve", lambda e, qb=qb: e.tensor_copy(out=gm[:, :, 0:qb], in_=bk[2][:, 0:64].rearrange("p (a b) -> p a b", a=2)[:, :, 0:qb]),
                     reads=[bk[2]], writes=[gm])
                for qt in range(2):
                    c.op("dve", lambda e, qt=qt: e.max(out=m8[:, qt, :], in_=gm[:, qt, :]), reads=[gm], writes=[m8])
                    c.op("dve", lambda e, qt=qt: e.tensor_scalar(out=selb[:, qt, :], in0=gm[:, qt, :], scalar1=m8[:, qt, 2:3], scalar2=-1.0,
                                                                 op0=ALU.is_ge, op1=ALU.add),
                         reads=[gm, m8], writes=[selb])
                    c.op("pe", lambda e, qt=qt: e.transpose(bkT[0:32, qt * 128:(qt + 1) * 128], selb[:, qt, :], identb[:]),
                         reads=[selb, identb], writes=[bkT])
                c.op("dve", lambda e: e.tensor_copy(out=selT[:], in_=bkT[0:32, 0:256]), reads=[bkT], writes=[selT])
            nblocks = qb + 1

            def emit_pv(n, pt):
                for kt in range(2):
                    first = (n == 0 and kt == 0)
                    last = (n == nblocks - 1 and kt == 1)
                    c.op("pe", lambda e, pt=pt, kt=kt, n=n, first=first, last=last: e.matmul(bk[4][:, 0:256], lhsT=Vt[:, n * 2 + kt, :], rhs=pt[:, kt, :],
                                                                                         start=first, stop=last),
                         reads=[Vt, pt], writes=[bk[4]])
                    c.op("pe", lambda e, pt=pt, kt=kt, first=first, last=last: e.matmul(bk[5][:, 0:256], lhsT=onesb[:], rhs=pt[:, kt, :],
                                                                                    start=first, stop=last),
                         reads=[onesb, pt], writes=[bk[5]])

            pend = None
            for n in range(nblocks):
                ps = bk[n % 2]
                pt = PT[ptc % 3]; ptc += 1
                own = (n == qb)
                for kt in range(2):
                    k0 = n * 256 + kt * 128
                    c.op("pe", lambda e, ps=ps, kt=kt, k0=k0, qs=qs: e.matmul(ps[:, kt * 256:(kt + 1) * 256], lhsT=KT[:, k0:k0 + 128], rhs=QT[:, qs],
                                                                              start=True, stop=False),
                         reads=[KT, QT], writes=[ps])
                    if own:
                        cm = cm0 if kt == 0 else cm1
                        c.op("pe", lambda e, ps=ps, kt=kt, cm=cm: e.matmul(ps[:, kt * 256:(kt + 1) * 256], lhsT=identb[:], rhs=cm[:],
                                                                           start=False, stop=True),
                             reads=[identb, cm], writes=[ps])
                    else:
                        c.op("pe", lambda e, ps=ps, kt=kt, n=n: e.matmul(ps[:, kt * 256:(kt + 1) * 256], lhsT=Eall[:, n, :], rhs=selT[:],
                                                                         start=False, stop=True),
                             reads=[Eall, selT], writes=[ps])
                c.op("act", lambda e, ps=ps, pt=pt: e.activation(out=pt[:], in_=ps[:].rearrange("p (a b) -> p a b", a=2), func=AF.Exp, scale=SCALE),
                     reads=[ps], writes=[pt])
                if pend is not None:
                    emit_pv(*pend)
                pend = (n, pt)
            emit_pv(*pend)
            c.op("dve", lambda e: e.reciprocal(out=rinv[:], in_=bk[5][:, 0:256]), reads=[bk[5]], writes=[rinv])
            c.op("dve", lambda e: e.tensor_tensor(out=on[:], in0=bk[4][:, 0:256], in1=rinv[:], op=ALU.mult), reads=[bk[4], rinv], writes=[on])
            obuf = ob[obc % 2]
            sub = qb % 4
            c.op("pool", lambda e, obuf=obuf, sub=sub, qs=qs: e.tensor_tensor(out=obuf[:, sub * 256:(sub + 1) * 256], in0=on[:], in1=ZS[:, qs], op=ALU.mult),
                 reads=[on, ZS], writes=[obuf])
            if sub == 3 or qb == NBLK - 1:
                q0 = (qb // 4) * 1024
                n_ = (sub + 1) * 256
                c.dma("sp", [(ogT[h * 128:(h + 1) * 128, q0:q0 + n_], obuf[:, 0:n_])], reads=[obuf], writes=[ogT], semT=obuf)
                obc += 1
    c.wait_all("sp", [ob[0], ob[1]])
    return c.finish()


def _run(nc, maps):
    res = run_bass_kernel_spmd(nc, maps, core_ids=list(range(len(maps))))
    return res.results


def kernel(**inputs):
    inp = {k: np.asarray(v) for k, v in inputs.items()}
    NT = 8192
    r1 = _run(build_s1(NT), [prep_s1(inp, c, NT) for c in range(8)])
    ynT = [np.concatenate([np.asarray(r1[b * 4 + j]["ynT"]) for j in range(4)], axis=0) for b in range(2)]
    g_kv = pcol(inp["kv_norm_g"], 16)
    g_b = pcol(inp["b_norm_g"][0], 16)
    w_out_a = np.ascontiguousarray(inp["a_w_out"][0])
    maps = []
    for c in range(8):
        b, j = c // 4, c % 4
        ts = slice(j * 2048, (j + 1) * 2048)
        maps.append({"aT": np.ascontiguousarray(ynT[b][:, ts]), "W": w_out_a,
                     "rT": np.ascontiguousarray(inp["x"][b, ts].T), "g1": g_kv, "g2": g_b})
    r2 = _run(build_proj(32, "s2", 2048, 256), maps)
    hkvT = [np.concatenate([np.asarray(r2[b * 4 + j]["hkvT"]) for j in range(4)], axis=1) for b in range(2)]
    hbT = [np.concatenate([np.asarray(r2[b * 4 + j]["hbT"]) for j in range(4)], axis=1) for b in range(2)]
    cosT, sinT = rope_tables(NT)
    w3 = [prep_w3(inp, j, 4) for j in range(4)]
    maps = []
    for c in range(8):
        b, j = c // 4, c % 4
        maps.append({"hkvT": hkvT[b], "hbT": hbT[b], "W3": w3[j], "cosT": cosT, "sinT": sinT})
    r3 = _run(build_s3(NT, 4), maps)
    ogT = [np.concatenate([np.asarray(r3[b * 4 + j]["ogT"]) for j in range(4)], axis=0) for b in range(2)]
    g_f = pcol(inp["final_norm_g"], 16)
    w_out_b = np.ascontiguousarray(inp["b_w_out"][0])
    maps = []
    for c in range(8):
        b, j = c // 4, c % 4
        ts = slice(j * 2048, (j + 1) * 2048)
        maps.append({"aT": np.ascontiguousarray(ogT[b][:, ts]), "W": w_out_b,
                     "rT": np.asarray(r2[c]["x1T"]), "g1": g_f})
    r4 = _run(build_proj(16, "s4", 2048, 512), maps)
    out = np.empty((2, NT, 2048), np.float32)
    for c in range(8):
        b, j = c // 4, c % 4
        out[b, j * 2048:(j + 1) * 2048, :] = np.asarray(r4[c]["outT"]).T
    return out
```

```python
import ml_dtypes
import numpy as np
import concourse.bass as bass
import concourse.mybir as mybir
from concourse.bass_utils import run_bass_kernel_spmd

F32 = mybir.dt.float32
BF16 = mybir.dt.bfloat16
U32 = mybir.dt.uint32
I32 = mybir.dt.int32
AF = mybir.ActivationFunctionType
ALU = mybir.AluOpType
AX = mybir.AxisListType

ENGS = ("pe", "dve", "act", "pool", "sp")


class T:
    def __init__(self, ctx, name, t, excl=False):
        self.ctx = ctx
        self.excl = excl
        self.name = name
        self.t = t
        self.last_write = None
        self.readers = {}
        self._sem = None
        self._semcnt = 0

    def __getitem__(self, idx):
        return self.t[idx]

    def dsem(self):
        if self._sem is None:
            self._sem = self.ctx.new_sem("d_" + self.name)
        return self._sem


class V:
    def __init__(self, T_, sl):
        self.T_ = T_
        self.sl = sl

    def __getitem__(self, idx):
        return self.T_[:, self.sl][idx]


class Ctx:
    def __init__(self, nc, same_engine_sync=True):
        self.nc = nc
        self.q = {e: [] for e in ENGS}
        self.cnt = {e: 0 for e in ENGS}
        self.sems = {}
        self.semh = {}
        for e in ENGS:
            self.semh[e] = nc.alloc_semaphore("s_" + e)
        self.waited = {e: {} for e in ENGS}
        self.same = same_engine_sync
        self.nsem = len(ENGS)
        self.uid = 0

    def new_sem(self, name):
        self.nsem += 1
        h = self.nc.alloc_semaphore(name)
        self.semh[name] = h
        return name

    def sb(self, name, shape, dt):
        return T(self, name, self.nc.alloc_sbuf_tensor(name, list(shape), dt))

    def ps(self, name, shape, dt=F32):
        return T(self, name, self.nc.alloc_psum_tensor(name, list(shape), dt), excl=True)

    def dram(self, name, shape, dt, kind=None):
        if kind is None:
            t = self.nc.dram_tensor(name, list(shape), dt)
        else:
            t = self.nc.dram_tensor(name, list(shape), dt, kind=kind)
        return T(self, name, t)

    def view(self, name, t):
        return T(self, name, t)

    def _deps(self, reads, writes):
        deps = {}
        writes = list(writes) + [t for t in reads if t.excl]
        reads = [t for t in reads if not t.excl]
        for t in reads:
            if t.last_write is not None:
                k, v = t.last_write
                deps[k] = max(deps.get(k, 0), v)
        for t in writes:
            if t.last_write is not None:
                k, v = t.last_write
                deps[k] = max(deps.get(k, 0), v)
            for k, v in t.readers.items():
                deps[k] = max(deps.get(k, 0), v)
        return deps

    def _emit_waits(self, e, deps, same):
        for k, v in deps.items():
            if k == e and not same:
                continue
            if self.waited[e].get(k, 0) < v:
                self.waited[e][k] = v
                h = self.semh[k]
                self.q[e].append(lambda eng, h=h, v=v: eng.wait_ge(h, v))

    def op(self, e, fn, reads=(), writes=(), same=None):
        if same is None:
            same = self.same and e != "pe"
        reads = [getattr(t, "T_", t) for t in reads]
        writes = [getattr(t, "T_", t) for t in writes]
        deps = self._deps(reads, writes)
        self._emit_waits(e, deps, same)
        self.cnt[e] += 1
        n = self.cnt[e]
        h = self.semh[e]
        self.q[e].append(lambda eng, fn=fn, h=h: fn(eng).then_inc(h, 1))
        for t in reads:
            if t.excl:
                t.last_write = (e, n)
                t.readers = {}
            else:
                t.readers[e] = max(t.readers.get(e, 0), n)
        for t in writes:
            t.last_write = (e, n)
            t.readers = {}
        return n

    def dma(self, e, pairs, reads=(), writes=(), semT=None, **kw):
        reads = [getattr(t, "T_", t) for t in reads]
        writes = [getattr(t, "T_", t) for t in writes]
        deps = self._deps(reads, writes)
        self._emit_waits(e, deps, True)
        if semT is None:
            semT = writes[0] if writes else reads[0]
        k = semT.dsem()
        h = self.semh[k]
        for (o, i) in pairs:
            self.q[e].append(lambda eng, o=o, i=i, h=h, kw=kw: eng.dma_start(out=o, in_=i, **kw).then_inc(h, 16))
        semT._semcnt += 16 * len(pairs)
        v = semT._semcnt
        for t in reads:
            t.readers[k] = max(t.readers.get(k, 0), v)
        for t in writes:
            t.last_write = (k, v)
            t.readers = {}
        return (k, v)

    def wait_all(self, e, ts):
        deps = {}
        for t in ts:
            if t.last_write is not None:
                k, v = t.last_write
                deps[k] = max(deps.get(k, 0), v)
            for k, v in t.readers.items():
                deps[k] = max(deps.get(k, 0), v)
        self._emit_waits(e, deps, True)

    def finish(self):
        nc = self.nc
        emap = {"pe": "tensor", "dve": "vector", "act": "scalar", "pool": "gpsimd", "sp": "sync"}
        with nc.Block() as block:
            for e in ENGS:
                thunks = self.q[e]
                if not thunks:
                    continue

                def body(eng, thunks=thunks):
                    for th in thunks:
                        th(eng)
                getattr(block, emap[e])(body)
        return nc


def pcol(v, n):
    return np.ascontiguousarray(np.asarray(v, np.float32).reshape(n, 128).T)


def prep_s1(inp, core, NT=8192):
    b, j = core // 4, core % 4
    g0 = 2 * j
    w = inp["a_w_in"][0]
    W1 = np.concatenate([
        w[:, 4096 + g0 * 512: 4096 + g0 * 512 + 1024],
        w[:, 8192 + g0 * 128: 8192 + g0 * 128 + 256],
        w[:, 9216 + g0 * 128: 9216 + g0 * 128 + 256],
        w[:, g0 * 512: g0 * 512 + 1024],
        w[:, 10240 + g0 * 8: 10240 + g0 * 8 + 16]], axis=1)
    chans = np.concatenate([np.arange(g0 * 512, g0 * 512 + 1024),
                            4096 + np.arange(g0 * 128, g0 * 128 + 256),
                            5120 + np.arange(g0 * 128, g0 * 128 + 256)])
    cwv = inp["a_conv_w"][0][:, chans]
    cw = np.ascontiguousarray(cwv.reshape(4, 12, 128).transpose(2, 1, 0).reshape(128, 48))
    cb = pcol(inp["a_conv_b"][0][chans], 12)
    hs = slice(g0 * 8, g0 * 8 + 16)
    dtb = np.ascontiguousarray(np.tile(inp["a_dt_bias"][0][hs][None, :], (128, 2)))
    alog = np.ascontiguousarray(np.tile(inp["a_A_log"][0][hs][None, :], (128, 2)))
    D = inp["a_D"][0][hs]
    dcol = np.ascontiguousarray(np.repeat(D.reshape(8, 2), 64, axis=1).T)
    gn = pcol(inp["a_gnorm_g"][0][g0 * 512: g0 * 512 + 1024], 8)
    ang = pcol(inp["a_norm_g"][0], 16)
    xT = np.ascontiguousarray(inp["x"][b, :NT].T)
    return {"xT": xT, "W1": np.ascontiguousarray(W1), "cw": cw, "cb": cb, "dtb": dtb.astype(np.float32),
            "alog": alog.astype(np.float32), "dcol": dcol.astype(np.float32), "gn": gn, "ang": ang}


def rope_tables(NT=8192):
    inv = (10000.0 ** (-np.arange(0, 128, 2, dtype=np.float32) / 128)).astype(np.float32)
    ang = np.arange(NT, dtype=np.float32)[:, None] * inv[None, :]
    cos = np.cos(ang).astype(np.float32).T
    sin = np.sin(ang).astype(np.float32).T
    cosT = np.ascontiguousarray(np.concatenate([cos, cos], axis=0))
    sinT = np.ascontiguousarray(np.concatenate([-sin, sin], axis=0))
    return cosT, sinT


def prep_w3(inp, j, NH=4):
    wkv = inp["w_kv"]
    wb = inp["b_w_in"][0]
    out = np.empty((NH, 2048, 768), np.float32)
    for i in range(NH):
        h = 4 * j + i
        cs = slice(h * 128, (h + 1) * 128)
        k = wkv[:, cs]
        q = wb[:, cs]
        out[i, :, 0:128] = k
        out[i, :, 128:256] = np.concatenate([k[:, 64:], k[:, :64]], axis=1)
        out[i, :, 256:384] = q
        out[i, :, 384:512] = np.concatenate([q[:, 64:], q[:, :64]], axis=1)
        out[i, :, 512:640] = wkv[:, 2048 + h * 128: 2048 + (h + 1) * 128]
        out[i, :, 640:768] = wb[:, 2048 + h * 128: 2048 + (h + 1) * 128]
    return out


EPS = 1e-5
BIG = 1e30


def build_s1(NT=8192):
    nc = bass.Bass("TRN2", target_bir_lowering=False)
    c = Ctx(nc)
    NCH = NT // 256
    xT = c.dram("xT", [2048, NT], F32, kind="ExternalInput")
    W1 = c.dram("W1", [2048, 2576], F32, kind="ExternalInput")
    cwd = c.dram("cw", [128, 48], F32, kind="ExternalInput")
    cbd = c.dram("cb", [128, 12], F32, kind="ExternalInput")
    dtbd = c.dram("dtb", [128, 32], F32, kind="ExternalInput")
    alogd = c.dram("alog", [128, 32], F32, kind="ExternalInput")
    dcold = c.dram("dcol", [128, 8], F32, kind="ExternalInput")
    gnd = c.dram("gn", [128, 8], F32, kind="ExternalInput")
    angd = c.dram("ang", [128, 16], F32, kind="ExternalInput")
    ynT = c.dram("ynT", [1024, NT], BF16, kind="ExternalOutput")

    W1b = c.sb("W1b", [128, 16, 2576], BF16)
    cw = c.sb("cws", [128, 48], F32)
    cb = c.sb("cbs", [128, 12], F32)
    dtb = c.sb("dtbs", [128, 32], F32)
    Aneg = c.sb("Aneg", [128, 32], F32)
    dcol = c.sb("dcols", [128, 8], F32)
    gn = c.sb("gns", [128, 8], F32)
    ang = c.sb("angs", [128, 16], F32)
    ones = c.sb("ones", [128, 128], F32)
    triA = c.sb("triA", [128, 256], F32)
    triB = c.sb("triB", [128, 256], F32)
    ident = c.sb("ident", [128, 128], BF16)

    c.dma("sp", [(cw[:], cwd[:]), (cb[:], cbd[:]), (dtb[:], dtbd[:]), (Aneg[:], alogd[:]),
                 (dcol[:], dcold[:]), (gn[:], gnd[:]), (ang[:], angd[:])],
          writes=[cw, cb, dtb, Aneg, dcol, gn, ang])
    pairs = []
    for ck in range(16):
        for hf in range(2):
            c0, c1 = hf * 1288, (hf + 1) * 1288
            pairs.append((W1b[:, ck, c0:c1], W1[ck * 128:(ck + 1) * 128, c0:c1]))
    c.dma("pool", pairs, writes=[W1b])
    c.op("pool", lambda e: e.memset(ones[:], 1.0), writes=[ones])
    c.op("pool", lambda e: e.memset(triA[:], 1.0), writes=[triA])
    c.op("pool", lambda e: e.affine_select(out=triA[:, 0:128], in_=triA[:, 0:128], pattern=[[1, 128]],
                                           compare_op=ALU.is_ge, fill=0.0, base=0, channel_multiplier=-1),
         reads=[triA], writes=[triA])
    c.op("pool", lambda e: e.memset(triB[:], 0.0), writes=[triB])
    c.op("pool", lambda e: e.tensor_copy(out=triB[:, 128:256], in_=triA[:, 0:128]), reads=[triA], writes=[triB])
    c.op("pool", lambda e: e.memset(ident[:], 1.0), writes=[ident])
    c.op("pool", lambda e: e.affine_select(out=ident[:], in_=ident[:], pattern=[[-1, 128]],
                                           compare_op=ALU.is_equal, fill=0.0, base=0, channel_multiplier=1),
         reads=[ident], writes=[ident])
    c.op("act", lambda e: e.activation(out=Aneg[:], in_=Aneg[:], func=AF.Exp), reads=[Aneg], writes=[Aneg])
    c.op("dve", lambda e: e.tensor_scalar(out=Aneg[:], in0=Aneg[:], scalar1=-1.0, scalar2=None, op0=ALU.mult),
         reads=[Aneg], writes=[Aneg])

    xTs = c.sb("xTs", [128, 16, 256], F32)
    sqr = [c.sb(f"sqr{i}", [128, 256], F32) for i in range(2)]
    rtmp = c.sb("rtmp", [128, 256], F32)
    rstd = c.sb("rstd", [128, 256], F32)
    hT = c.sb("hT", [128, 16, 256], BF16)
    u = c.sb("u", [128, 12, 259], F32)
    accr = [c.sb(f"acc{i}", [128, 256], F32) for i in range(2)]
    xcf = c.sb("xcf", [128, 8, 256], F32)
    xcb = c.sb("xcb", [128, 8, 256], BF16)
    BCb = c.sb("BCb", [128, 4, 256], BF16)
    zs = c.sb("zs", [128, 8, 256], F32)
    xtok = [c.sb(f"xtok{i}", [128, 1024], BF16) for i in range(2)]
    Btok = [c.sb(f"Btok{i}", [128, 256], BF16) for i in range(2)]
    xsc = [[c.sb(f"xsc{t}_{g}", [128, 512], BF16) for g in range(2)] for t in range(2)]
    dv = c.sb("dv", [128, 32], F32)
    dabs = c.sb("dabs", [128, 32], F32)
    dl = c.sb("dl", [128, 32], F32)
    dts = c.sb("dts", [128, 32], F32)
    asb = c.sb("asb", [128, 32], F32)
    lndt = c.sb("lndt", [128, 32], F32)
    bias = c.sb("bias", [128, 32], F32)
    cbm = [c.sb(f"cbm{g}", [128, 2, 256], F32) for g in range(2)]
    arep = [c.sb(f"arep{i}", [128, 2, 128], F32) for i in range(2)]
    E0 = [c.sb(f"E0_{i}", [128, 256], F32) for i in range(2)]
    E1 = [c.sb(f"E1_{i}", [128, 128], F32) for i in range(2)]
    Er = [c.sb(f"Er_{i}", [128, 256], F32) for i in range(2)]
    W0 = [c.sb(f"W0_{i}", [128, 256], BF16) for i in range(2)]
    Wt1 = [c.sb(f"Wt1_{i}", [128, 128], BF16) for i in range(2)]
    Cr = [c.sb(f"Cr_{i}", [128, 256], BF16) for i in range(2)]
    edec = [c.sb(f"edec{g}", [128, 8], F32) for g in range(2)]
    S = [c.sb(f"S{g}", [128, 512], F32) for g in range(2)]
    Sb = [c.sb(f"Sb{g}", [128, 512], BF16) for g in range(2)]
    ysb = [c.sb(f"ysb{i}", [128, 256], F32) for i in range(2)]
    yg = [c.sb(f"yg{g}", [128, 4, 256], F32) for g in range(2)]
    rg = c.sb("rg", [128, 256], F32)
    rgt = c.sb("rgt", [128, 256], F32)
    ynb = [c.sb(f"ynb{i}", [128, 8, 256], BF16) for i in range(2)]

    pj = [c.ps(f"pj{i}", [128, 512]) for i in range(2)]
    bmisc = c.ps("bmisc", [128, 512])
    bT = c.ps("bT", [128, 1024], BF16)
    bcb = c.ps("bcb", [128, 512])
    p_cum = [c.ps(f"pcum{i}", [128, 512]) for i in range(2)]
    bY = c.ps("bY", [128, 512])

    p_dt, p_ct, p_ss = V(bmisc, slice(0, 32)), V(bmisc, slice(32, 64)), V(bmisc, slice(256, 512))
    p_cb = p_S = bcb
    p_T = [V(bT, slice(0, 512)), V(bT, slice(512, 1024))]
    p_y = [V(bY, slice(0, 256)), V(bY, slice(256, 512))]

    for g in range(2):
        c.op("pool", lambda e, g=g: e.memset(S[g][:], 0.0), writes=[S[g]])
        c.op("pool", lambda e, g=g: e.memset(Sb[g][:], 0.0), writes=[Sb[g]])
    c.op("pool", lambda e: e.memset(u[:], 0.0), writes=[u])

    pjc = 0
    sqc = 0
    for t in range(NCH):
        t0 = t * 256
        src = xT[:, t0:t0 + 256].rearrange("(c p) t -> p c t", p=128)
        c.dma("sp", [(xTs[:, 4 * i:4 * i + 4, :], src[:, 4 * i:4 * i + 4, :]) for i in range(4)], writes=[xTs])
        for ck in range(16):
            sq = sqr[sqc % 2]; sqc += 1
            c.op("act", lambda e, sq=sq, ck=ck: e.activation(out=sq[:], in_=xTs[:, ck, :], func=AF.Square),
                 reads=[xTs], writes=[sq])
            c.op("pe", lambda e, sq=sq, ck=ck: e.matmul(p_ss[:], lhsT=ones[:], rhs=sq[:], start=(ck == 0), stop=(ck == 15)),
                 reads=[ones, sq], writes=[p_ss])
        c.op("act", lambda e: e.activation(out=rtmp[:], in_=p_ss[:], func=AF.Sqrt, scale=1.0 / 2048, bias=EPS),
             reads=[p_ss], writes=[rtmp])
        c.op("dve", lambda e: e.reciprocal(out=rstd[:], in_=rtmp[:]), reads=[rtmp], writes=[rstd])
        for ck in range(16):
            c.op("dve", lambda e, ck=ck: e.scalar_tensor_tensor(out=hT[:, ck, :], in0=xTs[:, ck, :], scalar=ang[:, ck:ck + 1],
                                                                in1=rstd[:], op0=ALU.mult, op1=ALU.mult),
                 reads=[xTs, ang, rstd], writes=[hT])
        for ot in range(20):
            pp = pj[pjc % 2]; pjc += 1
            for ck in range(16):
                c.op("pe", lambda e, pp=pp, ot=ot, ck=ck: e.matmul(pp[:, 0:256], lhsT=W1b[:, ck, ot * 128:(ot + 1) * 128], rhs=hT[:, ck, :],
                                                                   start=(ck == 0), stop=(ck == 15)),
                     reads=[W1b, hT], writes=[pp])
            if ot < 12:
                c.op("act", lambda e, pp=pp, ot=ot: e.copy(out=u[:, ot, 3:259], in_=pp[:, 0:256]), reads=[pp], writes=[u])
            else:
                c.op("act", lambda e, pp=pp, ot=ot: e.activation(out=zs[:, ot - 12, :], in_=pp[:, 0:256], func=AF.Silu),
                     reads=[pp], writes=[zs])
        for tt in range(2):
            for ck in range(16):
                c.op("pe", lambda e, tt=tt, ck=ck: e.matmul(p_dt[:, tt * 16:(tt + 1) * 16], lhsT=hT[:, ck, tt * 128:(tt + 1) * 128],
                                                            rhs=W1b[:, ck, 2560:2576], start=(ck == 0), stop=(ck == 15)),
                     reads=[W1b, hT], writes=[p_dt])
        c.op("dve", lambda e: e.tensor_tensor(out=dv[:], in0=p_dt[:], in1=dtb[:], op=ALU.add),
             reads=[p_dt, dtb], writes=[dv])
        c.op("act", lambda e: e.activation(out=dabs[:], in_=dv[:], func=AF.Abs), reads=[dv], writes=[dabs])
        c.op("act", lambda e: e.activation(out=dabs[:], in_=dabs[:], func=AF.Exp, scale=-1.0), reads=[dabs], writes=[dabs])
        c.op("act", lambda e: e.activation(out=dl[:], in_=dabs[:], func=AF.Ln, bias=1.0), reads=[dabs], writes=[dl])
        c.op("dve", lambda e: e.scalar_tensor_tensor(out=dts[:], in0=dv[:], scalar=0.0, in1=dl[:], op0=ALU.max, op1=ALU.add),
             reads=[dv, dl], writes=[dts])
        c.op("act", lambda e: e.activation(out=lndt[:], in_=dts[:], func=AF.Ln), reads=[dts], writes=[lndt])
        c.op("dve", lambda e: e.tensor_tensor(out=asb[:], in0=dts[:], in1=Aneg[:], op=ALU.mult),
             reads=[dts, Aneg], writes=[asb])
        c.op("pe", lambda e: e.matmul(p_ct[:, 0:16], lhsT=triA[:, 0:128], rhs=asb[:, 0:16], start=True, stop=True),
             reads=[triA, asb], writes=[p_ct])
        c.op("pe", lambda e: e.matmul(p_ct[:, 16:32], lhsT=ones[:], rhs=asb[:, 0:16], start=True, stop=False),
             reads=[ones, asb], writes=[p_ct])
        c.op("pe", lambda e: e.matmul(p_ct[:, 16:32], lhsT=triA[:, 0:128], rhs=asb[:, 16:32], start=False, stop=True),
             reads=[triA, asb], writes=[p_ct])
        c.op("dve", lambda e: e.tensor_tensor(out=bias[:], in0=lndt[:], in1=p_ct[:], op=ALU.subtract),
             reads=[lndt, p_ct], writes=[bias])
        for ot in range(12):
            acc = accr[ot % 2]
            c.op("dve", lambda e, acc=acc, ot=ot: e.tensor_scalar(out=acc[:], in0=u[:, ot, 3:259], scalar1=cw[:, ot * 4 + 3:ot * 4 + 4],
                                                                  scalar2=cb[:, ot:ot + 1], op0=ALU.mult, op1=ALU.add),
                 reads=[u, cw, cb], writes=[acc])
            for k in range(3):
                c.op("dve", lambda e, acc=acc, ot=ot, k=k: e.scalar_tensor_tensor(out=acc[:], in0=u[:, ot, k:k + 256],
                                                                                 scalar=cw[:, ot * 4 + k:ot * 4 + k + 1], in1=acc[:],
                                                                                 op0=ALU.mult, op1=ALU.add),
                     reads=[u, cw, acc], writes=[acc])
            if ot < 8:
                c.op("act", lambda e, acc=acc, ot=ot: e.activation(out=xcf[:, ot, :], in_=acc[:], func=AF.Silu),
                     reads=[acc], writes=[xcf])
            else:
                c.op("act", lambda e, acc=acc, ot=ot: e.activation(out=BCb[:, ot - 8, :], in_=acc[:], func=AF.Silu),
                     reads=[acc], writes=[BCb])
        c.op("pool", lambda e: e.tensor_copy(out=xcb[:], in_=xcf[:]), reads=[xcf], writes=[xcb])
        c.op("pool", lambda e: e.tensor_copy(out=u[:, :, 0:3], in_=u[:, :, 256:259]), reads=[u], writes=[u])
        for tt in range(2):
            for hf in range(2):
                for i in range(4):
                    ti = hf * 4 + i
                    c.op("pe", lambda e, tt=tt, hf=hf, i=i, ti=ti: e.transpose(p_T[hf][:, i * 128:(i + 1) * 128],
                                                                               xcb[:, ti, tt * 128:(tt + 1) * 128], ident[:]),
                         reads=[xcb, ident], writes=[p_T[hf]])
                c.op("dve", lambda e, tt=tt, hf=hf: e.tensor_copy(out=xtok[tt][:, hf * 512:(hf + 1) * 512], in_=p_T[hf][:]),
                     reads=[p_T[hf]], writes=[xtok[tt]])
            for g in range(2):
                c.op("pe", lambda e, tt=tt, g=g: e.transpose(p_T[0][:, g * 128:(g + 1) * 128], BCb[:, g, tt * 128:(tt + 1) * 128], ident[:]),
                     reads=[BCb, ident], writes=[p_T[0]])
            c.op("dve", lambda e, tt=tt: e.tensor_copy(out=Btok[tt][:], in_=p_T[0][:, 0:256]), reads=[p_T[0]], writes=[Btok[tt]])
        yo = ynb[t % 2]
        for g in range(2):
            BT = lambda sl, g=g: BCb[:, g, sl]
            for st in range(2):
                c.op("pe", lambda e, g=g, st=st: e.matmul(p_cb[:, st * 256:(st + 1) * 256], lhsT=BCb[:, g, st * 128:(st + 1) * 128],
                                                          rhs=BCb[:, 2 + g, :], start=True, stop=True),
                     reads=[BCb], writes=[p_cb])
            c.op("dve", lambda e, g=g: e.tensor_tensor(out=cbm[g][:, 0, :], in0=p_cb[:, 0:256], in1=triA[:], op=ALU.mult),
                 reads=[p_cb, triA], writes=[cbm[g]])
            c.op("dve", lambda e, g=g: e.tensor_tensor(out=cbm[g][:, 1, 128:256], in0=p_cb[:, 384:512], in1=triA[:, 0:128], op=ALU.mult),
                 reads=[p_cb, triA], writes=[cbm[g]])
            for r in range(8):
                hh = g * 8 + r
                par = hh % 2
                ar, pc = arep[par], p_cum[par]
                e0, e1, er, w0, w1, cr = E0[par], E1[par], Er[par], W0[par], Wt1[par], Cr[par]
                for st in range(2):
                    c.op("pool", lambda e, ar=ar, st=st, hh=hh: e.tensor_scalar(out=ar[:, st, :], in0=ones[:], scalar1=asb[:, st * 16 + hh:st * 16 + hh + 1],
                                                                                scalar2=0.0, op0=ALU.mult, op1=ALU.add),
                         reads=[ones, asb], writes=[ar])
                c.op("pe", lambda e, ar=ar, pc=pc: e.matmul(pc[:, 0:256], lhsT=ar[:, 0, :], rhs=triA[:], start=True, stop=False),
                     reads=[ar, triA], writes=[pc])
                c.op("pe", lambda e, ar=ar, pc=pc: e.matmul(pc[:, 0:256], lhsT=ar[:, 1, :], rhs=triB[:], start=False, stop=True),
                     reads=[ar, triB], writes=[pc])
                c.op("act", lambda e, e0=e0, pc=pc, hh=hh: e.activation(out=e0[:], in_=pc[:, 0:256], func=AF.Exp, bias=bias[:, hh:hh + 1]),
                     reads=[pc, bias], writes=[e0])
                c.op("act", lambda e, e1=e1, pc=pc, hh=hh: e.activation(out=e1[:], in_=pc[:, 128:256], func=AF.Exp, bias=bias[:, 16 + hh:16 + hh + 1]),
                     reads=[pc, bias], writes=[e1])
                c.op("act", lambda e, er=er, pc=pc: e.activation(out=er[:], in_=pc[:, 0:256], func=AF.Exp), reads=[pc], writes=[er])
                c.op("act", lambda e, pc=pc, g=g, r=r: e.activation(out=edec[g][:, r:r + 1], in_=pc[:, 255:256], func=AF.Exp),
                     reads=[pc], writes=[edec[g]])
                c.op("dve", lambda e, w0=w0, e0=e0, g=g: e.scalar_tensor_tensor(out=w0[:], in0=e0[:], scalar=BIG, in1=cbm[g][:, 0, :],
                                                                                op0=ALU.min, op1=ALU.mult),
                     reads=[e0, cbm[g]], writes=[w0])
                c.op("dve", lambda e, w1=w1, e1=e1, g=g: e.scalar_tensor_tensor(out=w1[:], in0=e1[:], scalar=BIG, in1=cbm[g][:, 1, 128:256],
                                                                                op0=ALU.min, op1=ALU.mult),
                     reads=[e1, cbm[g]], writes=[w1])
                c.op("pool", lambda e, cr=cr, er=er, g=g: e.tensor_tensor(out=cr[:], in0=BCb[:, 2 + g, :], in1=er[:], op=ALU.mult),
                     reads=[BCb, er], writes=[cr])
                c.op("pool", lambda e, e0=e0, g=g, r=r, hh=hh: e.tensor_scalar(out=xsc[0][g][:, r * 64:(r + 1) * 64],
                                                                               in0=xtok[0][:, hh * 64:(hh + 1) * 64], scalar1=e0[:, 255:256],
                                                                               scalar2=0.0, op0=ALU.mult, op1=ALU.add),
                     reads=[xtok[0], e0], writes=[xsc[0][g]])
                c.op("pool", lambda e, e1=e1, g=g, r=r, hh=hh: e.tensor_scalar(out=xsc[1][g][:, r * 64:(r + 1) * 64],
                                                                               in0=xtok[1][:, hh * 64:(hh + 1) * 64], scalar1=e1[:, 127:128],
                                                                               scalar2=0.0, op0=ALU.mult, op1=ALU.add),
                     reads=[xtok[1], e1], writes=[xsc[1][g]])
                py = p_y[(hh // 2) % 2]
                po = (r % 2) * 64
                c.op("pe", lambda e, py=py, po=po, w0=w0, hh=hh: e.matmul(py[po:po + 64, :], lhsT=xtok[0][:, hh * 64:(hh + 1) * 64], rhs=w0[:],
                                                                          start=True, stop=False),
                     reads=[xtok[0], w0], writes=[py])
                c.op("pe", lambda e, py=py, po=po, w1=w1, hh=hh: e.matmul(py[po:po + 64, 128:256], lhsT=xtok[1][:, hh * 64:(hh + 1) * 64], rhs=w1[:],
                                                                          start=False, stop=False),
                     reads=[xtok[1], w1], writes=[py])
                c.op("pe", lambda e, py=py, po=po, cr=cr, g=g, r=r: e.matmul(py[po:po + 64, :], lhsT=Sb[g][:, r * 64:(r + 1) * 64], rhs=cr[:],
                                                                             start=False, stop=True),
                     reads=[Sb[g], cr], writes=[py])
                if r % 2 == 1:
                    ti = r // 2
                    gt = g * 4 + ti
                    ys = ysb[ti % 2]
                    c.op("dve", lambda e, ys=ys, py=py, gt=gt: e.scalar_tensor_tensor(out=ys[:], in0=xcf[:, gt, :], scalar=dcol[:, gt:gt + 1],
                                                                                      in1=py[:], op0=ALU.mult, op1=ALU.add),
                         reads=[xcf, dcol, py], writes=[ys])
                    c.op("pool", lambda e, ys=ys, g=g, ti=ti, gt=gt: e.tensor_tensor(out=yg[g][:, ti, :], in0=ys[:], in1=zs[:, gt, :], op=ALU.mult),
                         reads=[ys, zs], writes=[yg[g]])
            c.op("pe", lambda e, g=g: e.matmul(p_S[:], lhsT=Btok[0][:, g * 128:(g + 1) * 128], rhs=xsc[0][g][:], start=True, stop=False),
                 reads=[Btok[0], xsc[0][g]], writes=[p_S])
            c.op("pe", lambda e, g=g: e.matmul(p_S[:], lhsT=Btok[1][:, g * 128:(g + 1) * 128], rhs=xsc[1][g][:], start=False, stop=True),
                 reads=[Btok[1], xsc[1][g]], writes=[p_S])
            for r in range(8):
                c.op("dve", lambda e, g=g, r=r: e.scalar_tensor_tensor(out=S[g][:, r * 64:(r + 1) * 64], in0=S[g][:, r * 64:(r + 1) * 64],
                                                                       scalar=edec[g][:, r:r + 1], in1=p_S[:, r * 64:(r + 1) * 64],
                                                                       op0=ALU.mult, op1=ALU.add),
                     reads=[S[g], edec[g], p_S], writes=[S[g]])
            c.op("pool", lambda e, g=g: e.tensor_copy(out=Sb[g][:], in_=S[g][:]), reads=[S[g]], writes=[Sb[g]])
            for ti in range(4):
                sq = sqr[sqc % 2]; sqc += 1
                c.op("act", lambda e, sq=sq, g=g, ti=ti: e.activation(out=sq[:], in_=yg[g][:, ti, :], func=AF.Square),
                     reads=[yg[g]], writes=[sq])
                c.op("pe", lambda e, sq=sq, ti=ti: e.matmul(p_ss[:], lhsT=ones[:], rhs=sq[:], start=(ti == 0), stop=(ti == 3)),
                     reads=[ones, sq], writes=[p_ss])
            c.op("act", lambda e: e.activation(out=rgt[:], in_=p_ss[:], func=AF.Sqrt, scale=1.0 / 512, bias=EPS),
                 reads=[p_ss], writes=[rgt])
            c.op("dve", lambda e: e.reciprocal(out=rg[:], in_=rgt[:]), reads=[rgt], writes=[rg])
            for ti in range(4):
                gt = g * 4 + ti
                c.op("dve", lambda e, g=g, ti=ti, gt=gt, yo=yo: e.scalar_tensor_tensor(out=yo[:, gt, :], in0=yg[g][:, ti, :], scalar=gn[:, gt:gt + 1],
                                                                                      in1=rg[:], op0=ALU.mult, op1=ALU.mult),
                     reads=[yg[g], gn, rg], writes=[yo])
        dst = ynT[:, t0:t0 + 256].rearrange("(c p) t -> p c t", p=128)
        c.dma("sp", [(dst, yo[:])], reads=[yo], writes=[ynT], semT=yo)
    c.wait_all("sp", [ynb[0], ynb[1]])
    for e in ("pe", "dve", "act", "pool"):
        pass
    return c.finish()


def build_proj(KC, mode, NTOK=2048, TB=256):
    nc = bass.Bass("TRN2", target_bir_lowering=False)
    c = Ctx(nc)
    NB = NTOK // TB
    aT = c.dram("aT", [KC * 128, NTOK], BF16, kind="ExternalInput")
    W = c.dram("W", [KC * 128, 2048], F32, kind="ExternalInput")
    rT = c.dram("rT", [2048, NTOK], F32, kind="ExternalInput")
    g1d = c.dram("g1", [128, 16], F32, kind="ExternalInput")
    g1 = c.sb("g1s", [128, 16], F32)
    if mode == "s2":
        g2d = c.dram("g2", [128, 16], F32, kind="ExternalInput")
        g2 = c.sb("g2s", [128, 16], F32)
        x1T = c.dram("x1T", [2048, NTOK], F32, kind="ExternalOutput")
        hkvT = c.dram("hkvT", [2048, NTOK], BF16, kind="ExternalOutput")
        hbT = c.dram("hbT", [2048, NTOK], BF16, kind="ExternalOutput")
        c.dma("sp", [(g1[:], g1d[:]), (g2[:], g2d[:])], writes=[g1, g2])
    else:
        outT = c.dram("outT", [2048, NTOK], F32, kind="ExternalOutput")
        c.dma("sp", [(g1[:], g1d[:])], writes=[g1])

    Wb = c.sb("Wb", [128, KC, 2048], BF16)
    ones = c.sb("ones", [128, 128], F32)
    pairs = []
    for ck in range(KC):
        pairs.append((Wb[:, ck, :], W[ck * 128:(ck + 1) * 128, :]))
    c.dma("pool", pairs, writes=[Wb])
    c.op("pool", lambda e: e.memset(ones[:], 1.0), writes=[ones])

    aTs = c.sb("aTs", [128, KC, TB], BF16)
    xs = c.sb("xs", [128, 16, TB], F32)
    sqr = [c.sb(f"sqr{i}", [128, TB], F32) for i in range(2)]
    rtmp = c.sb("rtmp", [128, TB], F32)
    rstd = c.sb("rstd", [128, TB], F32)
    if mode == "s2":
        o1 = [c.sb(f"o1_{i}", [128, 16, TB], BF16) for i in range(1)]
        o2 = [c.sb(f"o2_{i}", [128, 16, TB], BF16) for i in range(1)]
    else:
        o1 = [c.sb(f"o1_{i}", [128, 16, TB], F32) for i in range(1)]
    pj = [c.ps(f"pj{i}", [128, 512]) for i in range(2)]
    pss = c.ps("pss", [128, 512])

    def blk(d, b):
        return d[:, b * TB:(b + 1) * TB].rearrange("(c p) t -> p c t", p=128)

    sqc = 0
    pjc = 0
    for b in range(NB):
        sa = blk(aT, b)
        nsp = max(1, KC // 8)
        c.dma("sp", [(aTs[:, i * 8:(i + 1) * 8, :], sa[:, i * 8:(i + 1) * 8, :]) for i in range(nsp)], writes=[aTs])
        sr = blk(rT, b)
        c.dma("sp", [(xs[:, i * 8:(i + 1) * 8, :], sr[:, i * 8:(i + 1) * 8, :]) for i in range(2)], writes=[xs])
        for ft in range(16):
            pp = pj[pjc % 2]; pjc += 1
            for ck in range(KC):
                c.op("pe", lambda e, pp=pp, ft=ft, ck=ck: e.matmul(pp[:, 0:TB], lhsT=Wb[:, ck, ft * 128:(ft + 1) * 128], rhs=aTs[:, ck, :],
                                                                   start=(ck == 0), stop=(ck == KC - 1)),
                     reads=[Wb, aTs], writes=[pp])
            c.op("dve", lambda e, pp=pp, ft=ft: e.tensor_tensor(out=xs[:, ft, :], in0=pp[:, 0:TB], in1=xs[:, ft, :], op=ALU.add),
                 reads=[pp, xs], writes=[xs])
            sq = sqr[sqc % 2]; sqc += 1
            c.op("act", lambda e, sq=sq, ft=ft: e.activation(out=sq[:], in_=xs[:, ft, :], func=AF.Square), reads=[xs], writes=[sq])
            c.op("pe", lambda e, sq=sq, ft=ft: e.matmul(pss[:, 0:TB], lhsT=ones[:], rhs=sq[:], start=(ft == 0), stop=(ft == 15)),
                 reads=[ones, sq], writes=[pss])
        c.op("act", lambda e: e.activation(out=rtmp[:], in_=pss[:, 0:TB], func=AF.Sqrt, scale=1.0 / 2048, bias=EPS),
             reads=[pss], writes=[rtmp])
        c.op("dve", lambda e: e.reciprocal(out=rstd[:], in_=rtmp[:]), reads=[rtmp], writes=[rstd])
        oa = o1[0]
        for ft in range(16):
            c.op("dve", lambda e, ft=ft, oa=oa: e.scalar_tensor_tensor(out=oa[:, ft, :], in0=xs[:, ft, :], scalar=g1[:, ft:ft + 1], in1=rstd[:],
                                                                       op0=ALU.mult, op1=ALU.mult),
                 reads=[xs, g1, rstd], writes=[oa])
        if mode == "s2":
            ob = o2[0]
            for ft in range(16):
                c.op("dve", lambda e, ft=ft, ob=ob: e.scalar_tensor_tensor(out=ob[:, ft, :], in0=xs[:, ft, :], scalar=g2[:, ft:ft + 1], in1=rstd[:],
                                                                           op0=ALU.mult, op1=ALU.mult),
                     reads=[xs, g2, rstd], writes=[ob])
            c.dma("sp", [(blk(x1T, b), xs[:])], reads=[xs], writes=[x1T], semT=xs)
            c.dma("sp", [(blk(hkvT, b), oa[:])], reads=[oa], writes=[hkvT], semT=oa)
            c.dma("sp", [(blk(hbT, b), ob[:])], reads=[ob], writes=[hbT], semT=ob)
        else:
            c.dma("sp", [(blk(outT, b), oa[:])], reads=[oa], writes=[outT], semT=oa)
    if mode == "s2":
        c.wait_all("sp", [xs, o1[0], o2[0]])
    else:
        c.wait_all("sp", [o1[0]])
    return c.finish()


BIGM = 30000.0
SCALE = 128 ** -0.5


def build_s3(NT=8192, NH=4):
    nc = bass.Bass("TRN2", target_bir_lowering=False)
    c = Ctx(nc)
    NBLK = NT // 256
    NTB = NT // 512
    hkvT = c.dram("hkvT", [2048, NT], BF16, kind="ExternalInput")
    hbT = c.dram("hbT", [2048, NT], BF16, kind="ExternalInput")
    W3 = c.dram("W3", [NH, 2048, 768], F32, kind="ExternalInput")
    cosd = c.dram("cosT", [128, NT], F32, kind="ExternalInput")
    sind = c.dram("sinT", [128, NT], F32, kind="ExternalInput")
    ogT = c.dram("ogT", [NH * 128, NT], BF16, kind="ExternalOutput")

    onesb = c.sb("onesb", [128, 128], BF16)
    identb = c.sb("identb", [128, 128], BF16)
    Eall = c.sb("Eall", [32, 32, 128], BF16)
    cm0 = c.sb("cm0", [128, 256], BF16)
    cm1 = c.sb("cm1", [128, 256], BF16)
    c.op("pool", lambda e: e.memset(onesb[:], 1.0), writes=[onesb])
    c.op("pool", lambda e: e.memset(identb[:], 1.0), writes=[identb])
    c.op("pool", lambda e: e.affine_select(out=identb[:], in_=identb[:], pattern=[[-1, 128]], compare_op=ALU.is_equal,
                                           fill=0.0, base=0, channel_multiplier=1), reads=[identb], writes=[identb])
    c.op("pool", lambda e: e.memset(Eall[:], BIGM), writes=[Eall])
    c.op("pool", lambda e: e.affine_select(out=Eall[:], in_=Eall[:], pattern=[[-1, 32], [0, 128]], compare_op=ALU.is_equal,
                                           fill=0.0, base=0, channel_multiplier=1), reads=[Eall], writes=[Eall])
    c.op("pool", lambda e: e.memset(cm0[:], 0.0), writes=[cm0])
    c.op("pool", lambda e: e.affine_select(out=cm0[:], in_=cm0[:], pattern=[[1, 256]], compare_op=ALU.is_ge,
                                           fill=-BIGM, base=0, channel_multiplier=-1), reads=[cm0], writes=[cm0])
    c.op("pool", lambda e: e.memset(cm1[:], 0.0), writes=[cm1])
    c.op("pool", lambda e: e.affine_select(out=cm1[:], in_=cm1[:], pattern=[[1, 256]], compare_op=ALU.is_ge,
                                           fill=-BIGM, base=-128, channel_multiplier=-1), reads=[cm1], writes=[cm1])

    Wh = c.sb("Wh", [128, 16, 768], BF16)
    hkv2 = [c.sb(f"hkv{i}", [128, 16, 512], BF16) for i in range(2)]
    hb2 = [c.sb(f"hb{i}", [128, 16, 512], BF16) for i in range(2)]
    cosb2 = [c.sb(f"cosb{i}", [128, 512], F32) for i in range(2)]
    sinb2 = [c.sb(f"sinb{i}", [128, 512], F32) for i in range(2)]
    KT = c.sb("KT", [128, NT], BF16)
    QT = c.sb("QT", [128, NT], BF16)
    Vt = c.sb("Vt", [128, NT // 128, 128], BF16)
    ZS = c.sb("ZS", [128, NT], BF16)
    t1 = [c.sb(f"t1_{i}", [128, 512], F32) for i in range(2)]
    t2 = [c.sb(f"t2_{i}", [128, 512], F32) for i in range(2)]
    kr = c.sb("kr", [128, 512], F32)
    kbs = c.sb("kbs", [128, NBLK], F32)
    kbb = c.sb("kbb", [128, 32], BF16)
    gm = c.sb("gm", [128, 2, 32], F32)
    m8 = c.sb("m8", [128, 2, 8], F32)
    selb = c.sb("selb", [128, 2, 32], BF16)
    selT = c.sb("selT", [32, 256], BF16)
    PT = [c.sb(f"PT{i}", [128, 2, 256], BF16) for i in range(3)]
    rinv = c.sb("rinv", [128, 256], F32)
    on = c.sb("on", [128, 256], F32)
    ob = [c.sb(f"ob{i}", [128, 1024], BF16) for i in range(2)]

    bk = [c.ps(f"bk{i}", [128, 512]) for i in range(7)]
    bkT = c.ps("bkT", [128, 1024], BF16)

    def tblk(d, tb):
        return d[:, tb * 512:(tb + 1) * 512].rearrange("(c p) t -> p c t", p=128)

    obc = 0
    ptc = 0
    for h in range(NH):
        c.dma("pool", [(Wh[:, ck, :], W3[h, ck * 128:(ck + 1) * 128, :]) for ck in range(16)], writes=[Wh])
        c.op("pool", lambda e: e.memset(kbb[:], 0.0), writes=[kbb])
        c.op("pool", lambda e: e.memset(gm[:], -1e30), writes=[gm])
        for tb in range(NTB):
            s_kv, s_b = tblk(hkvT, tb), tblk(hbT, tb)
            hkv, hb, cosb, sinb = hkv2[tb % 2], hb2[tb % 2], cosb2[tb % 2], sinb2[tb % 2]
            c.dma("sp", [(hkv[:, 0:8, :], s_kv[:, 0:8, :]), (hkv[:, 8:16, :], s_kv[:, 8:16, :])], writes=[hkv])
            c.dma("sp", [(hb[:, 0:8, :], s_b[:, 0:8, :]), (hb[:, 8:16, :], s_b[:, 8:16, :])], writes=[hb])
            c.dma("sp", [(cosb[:], cosd[:, tb * 512:(tb + 1) * 512])], writes=[cosb])
            c.dma("sp", [(sinb[:], sind[:, tb * 512:(tb + 1) * 512])], writes=[sinb])
            ts = slice(tb * 512, (tb + 1) * 512)
            for which, src, dstT, w0 in (("k", hkv, KT, 0), ("q", hb, QT, 256)):
                pa, pb = (bk[0], bk[1]) if which == "k" else (bk[2], bk[3])
                for ck in range(16):
                    c.op("pe", lambda e, pa=pa, src=src, ck=ck, w0=w0: e.matmul(pa[:], lhsT=Wh[:, ck, w0:w0 + 128], rhs=src[:, ck, :],
                                                                                start=(ck == 0), stop=(ck == 15)),
                         reads=[Wh, src], writes=[pa])
                for ck in range(16):
                    c.op("pe", lambda e, pb=pb, src=src, ck=ck, w0=w0: e.matmul(pb[:], lhsT=Wh[:, ck, w0 + 128:w0 + 256], rhs=src[:, ck, :],
                                                                                start=(ck == 0), stop=(ck == 15)),
                         reads=[Wh, src], writes=[pb])
                i = 0 if which == "k" else 1
                c.op("dve", lambda e, pa=pa, i=i, cosb=cosb: e.tensor_tensor(out=t1[i][:], in0=pa[:], in1=cosb[:], op=ALU.mult),
                     reads=[pa, cosb], writes=[t1[i]])
                c.op("dve", lambda e, pb=pb, i=i, sinb=sinb: e.tensor_tensor(out=t2[i][:], in0=pb[:], in1=sinb[:], op=ALU.mult),
                     reads=[pb, sinb], writes=[t2[i]])
                if which == "k":
                    c.op("pool", lambda e, i=i: e.tensor_tensor(out=kr[:], in0=t1[i][:], in1=t2[i][:], op=ALU.add),
                         reads=[t1[i], t2[i]], writes=[kr])
                    c.op("act", lambda e, ts=ts: e.copy(out=KT[:, ts], in_=kr[:]), reads=[kr], writes=[KT])
                    c.op("dve", lambda e, tb=tb: e.tensor_reduce(out=kbs[:, tb * 2:tb * 2 + 2], in_=kr[:].rearrange("p (a b) -> p a b", a=2),
                                                                 axis=AX.X, op=ALU.add),
                         reads=[kr], writes=[kbs])
                else:
                    c.op("pool", lambda e, i=i, ts=ts: e.tensor_tensor(out=QT[:, ts], in0=t1[i][:], in1=t2[i][:], op=ALU.add),
                         reads=[t1[i], t2[i]], writes=[QT])
            for tt in range(4):
                for ck in range(16):
                    c.op("pe", lambda e, tt=tt, ck=ck, hkv=hkv: e.matmul(bk[4][:, tt * 128:(tt + 1) * 128], lhsT=hkv[:, ck, tt * 128:(tt + 1) * 128],
                                                                rhs=Wh[:, ck, 512:640], start=(ck == 0), stop=(ck == 15)),
                         reads=[Wh, hkv], writes=[bk[4]])
            c.op("act", lambda e, tb=tb: e.copy(out=Vt[:, tb * 4:(tb + 1) * 4, :], in_=bk[4][:].rearrange("p (a b) -> p a b", a=4)),
                 reads=[bk[4]], writes=[Vt])
            for ck in range(16):
                c.op("pe", lambda e, ck=ck, hb=hb: e.matmul(bk[5][:], lhsT=Wh[:, ck, 640:768], rhs=hb[:, ck, :], start=(ck == 0), stop=(ck == 15)),
                     reads=[Wh, hb], writes=[bk[5]])
            c.op("act", lambda e, ts=ts: e.activation(out=ZS[:, ts], in_=bk[5][:], func=AF.Silu), reads=[bk[5]], writes=[ZS])
        c.op("act", lambda e: e.activation(out=kbb[:, 0:NBLK], in_=kbs[:], func=AF.Copy, scale=1.0 / 256), reads=[kbs], writes=[kbb])
        for qb in range(NBLK):
            qs = slice(qb * 256, (qb + 1) * 256)
            if qb > 0:
                for qt in range(2):
                    c.op("pe", lambda e, qt=qt, qb=qb: e.matmul(bk[2][:, qt * 32:(qt + 1) * 32], lhsT=QT[:, qb * 256 + qt * 128:qb * 256 + (qt + 1) * 128],
                                                                rhs=kbb[:], start=True, stop=True),
                         reads=[QT, kbb], writes=[bk[2]])
                c.op("dve", lambda e, qb=qb: e.tensor_copy(out=gm[:, :, 0:qb], in_=bk[2][:, 0:64].rearrange("p (a b) -> p a b", a=2)[:, :, 0:qb]),
                     reads=[bk[2]], writes=[gm])
                for qt in range(2):
                    c.op("dve", lambda e, qt=qt: e.max(out=m8[:, qt, :], in_=gm[:, qt, :]), reads=[gm], writes=[m8])
                    c.op("dve", lambda e, qt=qt: e.tensor_scalar(out=selb[:, qt, :], in0=gm[:, qt, :], scalar1=m8[:, qt, 2:3], scalar2=-1.0,
                                                                 op0=ALU.is_ge, op1=ALU.add),
                         reads=[gm, m8], writes=[selb])
                    c.op("pe", lambda e, qt=qt: e.transpose(bkT[0:32, qt * 128:(qt + 1) * 128], selb[:, qt, :], identb[:]),
                         reads=[selb, identb], writes=[bkT])
                c.op("dve", lambda e: e.tensor_copy(out=selT[:], in_=bkT[0:32, 0:256]), reads=[bkT], writes=[selT])
            nblocks = qb + 1

            def emit_pv(n, pt):
                for kt in range(2):
                    first = (n == 0 and kt == 0)
                    last = (n == nblocks - 1 and kt == 1)
                    c.op("pe", lambda e, pt=pt, kt=kt, n=n, first=first, last=last: e.matmul(bk[4][:, 0:256], lhsT=Vt[:, n * 2 + kt, :], rhs=pt[:, kt, :],
                                                                                         start=first, stop=last),
                         reads=[Vt, pt], writes=[bk[4]])
                    c.op("pe", lambda e, pt=pt, kt=kt, first=first, last=last: e.matmul(bk[5][:, 0:256], lhsT=onesb[:], rhs=pt[:, kt, :],
                                                                                    start=first, stop=last),
                         reads=[onesb, pt], writes=[bk[5]])

            pend = None
            for n in range(nblocks):
                ps = bk[n % 2]
                pt = PT[ptc % 3]; ptc += 1
                own = (n == qb)
                for kt in range(2):
                    k0 = n * 256 + kt * 128
                    c.op("pe", lambda e, ps=ps, kt=kt, k0=k0, qs=qs: e.matmul(ps[:, kt * 256:(kt + 1) * 256], lhsT=KT[:, k0:k0 + 128], rhs=QT[:, qs],
                                                                              start=True, stop=False),
                         reads=[KT, QT], writes=[ps])
                    if own:
                        cm = cm0 if kt == 0 else cm1
                        c.op("pe", lambda e, ps=ps, kt=kt, cm=cm: e.matmul(ps[:, kt * 256:(kt + 1) * 256], lhsT=identb[:], rhs=cm[:],
                                                                           start=False, stop=True),
                             reads=[identb, cm], writes=[ps])
                    else:
                        c.op("pe", lambda e, ps=ps, kt=kt, n=n: e.matmul(ps[:, kt * 256:(kt + 1) * 256], lhsT=Eall[:, n, :], rhs=selT[:],
                                                                         start=False, stop=True),
                             reads=[Eall, selT], writes=[ps])
                c.op("act", lambda e, ps=ps, pt=pt: e.activation(out=pt[:], in_=ps[:].rearrange("p (a b) -> p a b", a=2), func=AF.Exp, scale=SCALE),
                     reads=[ps], writes=[pt])
                if pend is not None:
                    emit_pv(*pend)
                pend = (n, pt)
            emit_pv(*pend)
            c.op("dve", lambda e: e.reciprocal(out=rinv[:], in_=bk[5][:, 0:256]), reads=[bk[5]], writes=[rinv])
            c.op("dve", lambda e: e.tensor_tensor(out=on[:], in0=bk[4][:, 0:256], in1=rinv[:], op=ALU.mult), reads=[bk[4], rinv], writes=[on])
            obuf = ob[obc % 2]
            sub = qb % 4
            c.op("pool", lambda e, obuf=obuf, sub=sub, qs=qs: e.tensor_tensor(out=obuf[:, sub * 256:(sub + 1) * 256], in0=on[:], in1=ZS[:, qs], op=ALU.mult),
                 reads=[on, ZS], writes=[obuf])
            if sub == 3 or qb == NBLK - 1:
                q0 = (qb // 4) * 1024
                n_ = (sub + 1) * 256
                c.dma("sp", [(ogT[h * 128:(h + 1) * 128, q0:q0 + n_], obuf[:, 0:n_])], reads=[obuf], writes=[ogT], semT=obuf)
                obc += 1
    c.wait_all("sp", [ob[0], ob[1]])
    return c.finish()


def _run(nc, maps):
    res = run_bass_kernel_spmd(nc, maps, core_ids=list(range(len(maps))))
    return res.results


def kernel(**inputs):
    inp = {k: np.asarray(v) for k, v in inputs.items()}
    NT = 8192
    r1 = _run(build_s1(NT), [prep_s1(inp, c, NT) for c in range(8)])
    ynT = [np.concatenate([np.asarray(r1[b * 4 + j]["ynT"]) for j in range(4)], axis=0) for b in range(2)]
    g_kv = pcol(inp["kv_norm_g"], 16)
    g_b = pcol(inp["b_norm_g"][0], 16)
    w_out_a = np.ascontiguousarray(inp["a_w_out"][0])
    maps = []
    for c in range(8):
        b, j = c // 4, c % 4
        ts = slice(j * 2048, (j + 1) * 2048)
        maps.append({"aT": np.ascontiguousarray(ynT[b][:, ts]), "W": w_out_a,
                     "rT": np.ascontiguousarray(inp["x"][b, ts].T), "g1": g_kv, "g2": g_b})
    r2 = _run(build_proj(32, "s2", 2048, 256), maps)
    hkvT = [np.concatenate([np.asarray(r2[b * 4 + j]["hkvT"]) for j in range(4)], axis=1) for b in range(2)]
    hbT = [np.concatenate([np.asarray(r2[b * 4 + j]["hbT"]) for j in range(4)], axis=1) for b in range(2)]
    cosT, sinT = rope_tables(NT)
    w3 = [prep_w3(inp, j, 4) for j in range(4)]
    maps = []
    for c in range(8):
        b, j = c // 4, c % 4
        maps.append({"hkvT": hkvT[b], "hbT": hbT[b], "W3": w3[j], "cosT": cosT, "sinT": sinT})
    r3 = _run(build_s3(NT, 4), maps)
    ogT = [np.concatenate([np.asarray(r3[b * 4 + j]["ogT"]) for j in range(4)], axis=0) for b in range(2)]
    g_f = pcol(inp["final_norm_g"], 16)
    w_out_b = np.ascontiguousarray(inp["b_w_out"][0])
    maps = []
    for c in range(8):
        b, j = c // 4, c % 4
        ts = slice(j * 2048, (j + 1) * 2048)
        maps.append({"aT": np.ascontiguousarray(ogT[b][:, ts]), "W": w_out_b,
                     "rT": np.asarray(r2[c]["x1T"]), "g1": g_f})
    r4 = _run(build_proj(16, "s4", 2048, 512), maps)
    out = np.empty((2, NT, 2048), np.float32)
    for c in range(8):
        b, j = c // 4, c % 4
        out[b, j * 2048:(j + 1) * 2048, :] = np.asarray(r4[c]["outT"]).T
    return out
```
